# Optimizing a Trainium2 kernel written in Bass

```python
import jax, jax.numpy as jnp
from jax import lax
import numpy as np

D_MODEL = 2048
BATCH = 4
SEQ = 4096
DEPTH = 2

CHUNK = 64
N_MIXERS = 2
EPS = 1e-6

SSM_EXPAND = 2
D_INNER = SSM_EXPAND * D_MODEL
SSM_HEADDIM = 64
SSM_HEADS = D_INNER // SSM_HEADDIM
SSM_GROUPS = 8
SSM_HEADS_PER_GROUP = SSM_HEADS // SSM_GROUPS
SSM_STATE = 128
SSM_CONV = 4
SSD_CHUNK = CHUNK
CONV_DIM = D_INNER + 2 * SSM_GROUPS * SSM_STATE
SSM_IN = D_INNER + CONV_DIM + SSM_HEADS
DT_MIN = 1e-3
DT_MAX = 1e-1

POOL_EXPAND = 2
D_POOL = POOL_EXPAND * D_MODEL
POOL_WINDOWS = (2, 4, 8, 16)
POOL_GROUPS = len(POOL_WINDOWS)
POOL_GROUP_DIM = D_POOL // POOL_GROUPS

kernel_name = "hybrid_ssd_multiscale_pool_trunk"


def rmsnorm(x, g):
    xf = x.astype(jnp.float32)
    y = xf * lax.rsqrt(jnp.mean(xf * xf, axis=-1, keepdims=True) + EPS)
    return (y * g.astype(jnp.float32)).astype(x.dtype)


def causal_depthwise_conv(x, w, b):
    k_taps = w.shape[0]
    length = x.shape[1]
    xp = jnp.pad(x, ((0, 0), (k_taps - 1, 0), (0, 0)))
    y = b
    for k in range(k_taps):
        y = y + xp[:, k:k + length] * w[k]
    return y


def ssd_chunked_scan(xdt, a_dt, bm, cm):
    bsz, length, g, r, p = xdt.shape
    n = bm.shape[-1]
    n_chunks = length // SSD_CHUNK

    def to_chunks(t):
        t = t.reshape((bsz, n_chunks, SSD_CHUNK) + t.shape[2:])
        return jnp.moveaxis(t, 1, 0)

    mask = jnp.tril(jnp.ones((SSD_CHUNK, SSD_CHUNK), dtype=bool))[None, :, :, None, None]

    def step(state, inp):
        xc, ac, bc, cc = inp
        cs = jnp.cumsum(ac, axis=1)
        seg = cs[:, :, None] - cs[:, None, :]
        decay = jnp.exp(jnp.where(mask, seg, -jnp.inf))
        cb = jnp.einsum("blgn,bsgn->blsg", cc, bc)
        y_diag = jnp.einsum("blsg,blsgr,bsgrp->blgrp", cb, decay, xc)
        y_off = jnp.einsum("blgn,bgrpn,blgr->blgrp", cc, state, jnp.exp(cs))
        last = cs[:, -1]
        w_in = jnp.exp(last[:, None] - cs)
        new_state = state * jnp.exp(last)[..., None, None] + jnp.einsum(
            "bsgn,bsgr,bsgrp->bgrpn", bc, w_in, xc)
        return new_state, y_diag + y_off

    state0 = jnp.zeros((bsz, g, r, p, n), jnp.float32)
    _, ys = lax.scan(step, state0, (to_chunks(xdt), to_chunks(a_dt), to_chunks(bm), to_chunks(cm)))
    return jnp.moveaxis(ys, 0, 1).reshape(bsz, length, g, r, p)


def ssm_mixer(h, w_in, conv_w, conv_b, dt_bias, a_log, d_skip, norm_g, w_out):
    bsz, length, _ = h.shape
    g, r, p, n = SSM_GROUPS, SSM_HEADS_PER_GROUP, SSM_HEADDIM, SSM_STATE
    proj = h @ w_in
    z = proj[..., :D_INNER]
    xbc = proj[..., D_INNER:D_INNER + CONV_DIM]
    dt = proj[..., D_INNER + CONV_DIM:]
    xbc = jax.nn.silu(causal_depthwise_conv(xbc, conv_w, conv_b)).astype(jnp.float32)
    xs = xbc[..., :D_INNER].reshape(bsz, length, g, r, p)
    bm = xbc[..., D_INNER:D_INNER + g * n].reshape(bsz, length, g, n)
    cm = xbc[..., D_INNER + g * n:].reshape(bsz, length, g, n)
    dt = jax.nn.softplus(dt.astype(jnp.float32) + dt_bias.astype(jnp.float32))
    dt = dt.reshape(bsz, length, g, r)
    a = -jnp.exp(a_log.astype(jnp.float32)).reshape(g, r)
    y = ssd_chunked_scan(xs * dt[..., None], dt * a, bm, cm)
    y = y + d_skip.astype(jnp.float32).reshape(g, r, 1) * xs
    y = y.reshape(bsz, length, D_INNER) * jax.nn.silu(z.astype(jnp.float32))
    y = rmsnorm(y, norm_g)
    return y.astype(h.dtype) @ w_out


def pool_mixer(h, w_in, w_group, scale, w_out):
    bsz, length, _ = h.shape
    proj = h @ w_in
    u = proj[..., :D_POOL].astype(jnp.float32).reshape(bsz, length, POOL_GROUPS, POOL_GROUP_DIM)
    gate = proj[..., D_POOL:].astype(jnp.float32)
    cs = jnp.cumsum(u, axis=1)
    pos = jnp.arange(1, length + 1, dtype=jnp.int32)
    means = []
    for gi, win in enumerate(POOL_WINDOWS):
        c = cs[:, :, gi]
        shifted = jnp.pad(c, ((0, 0), (win, 0), (0, 0)))[:, :length]
        cnt = jnp.minimum(pos, win).astype(jnp.float32)[None, :, None]
        means.append((c - shifted) / cnt)
    mixed = jnp.stack(means, axis=2) - u
    mixed = jnp.einsum("blgc,gcd->blgd", mixed, w_group.astype(jnp.float32))
    mixed = mixed.reshape(bsz, length, D_POOL) * scale.astype(jnp.float32)
    y = mixed * jax.nn.silu(gate)
    return y.astype(h.dtype) @ w_out


def setup_inputs(seed: int = 0) -> dict:
    key = jax.random.key(seed)
    ks = jax.random.split(key, 20)
    n_a = (DEPTH + N_MIXERS - 1) // N_MIXERS
    n_b = DEPTH // N_MIXERS
    f32 = jnp.float32
    x = jax.random.normal(ks[0], (BATCH, SEQ, D_MODEL), f32)
    ln_g = 1.0 + 0.05 * jax.random.normal(ks[1], (DEPTH, D_MODEL), f32)
    final_g = 1.0 + 0.05 * jax.random.normal(ks[2], (D_MODEL,), f32)
    ssm_w_in = jax.random.normal(ks[3], (n_a, D_MODEL, SSM_IN), f32) * D_MODEL ** -0.5
    ssm_conv_w = jax.random.normal(ks[4], (n_a, SSM_CONV, CONV_DIM), f32) * SSM_CONV ** -0.5
    ssm_conv_b = 0.01 * jax.random.normal(ks[5], (n_a, CONV_DIM), f32)
    u_dt = jax.random.uniform(ks[6], (n_a, SSM_HEADS), f32)
    dt0 = jnp.exp(u_dt * (np.log(DT_MAX) - np.log(DT_MIN)) + np.log(DT_MIN))
    ssm_dt_bias = dt0 + jnp.log(-jnp.expm1(-dt0))
    ssm_a_log = jnp.log(jax.random.uniform(ks[7], (n_a, SSM_HEADS), f32, 1.0, 16.0))
    ssm_d = 1.0 + 0.1 * jax.random.normal(ks[8], (n_a, SSM_HEADS), f32)
    ssm_norm_g = 1.0 + 0.05 * jax.random.normal(ks[9], (n_a, D_INNER), f32)
    ssm_w_out = jax.random.normal(ks[10], (n_a, D_INNER, D_MODEL), f32) * D_INNER ** -0.5
    pool_w_in = jax.random.normal(ks[11], (n_b, D_MODEL, 2 * D_POOL), f32) * D_MODEL ** -0.5
    pool_w_group = jax.random.normal(ks[12], (n_b, POOL_GROUPS, POOL_GROUP_DIM, POOL_GROUP_DIM), f32) * POOL_GROUP_DIM ** -0.5
    pool_scale = 1.0 + 0.1 * jax.random.normal(ks[13], (n_b, D_POOL), f32)
    pool_w_out = jax.random.normal(ks[14], (n_b, D_POOL, D_MODEL), f32) * D_POOL ** -0.5
    return {"x": x, "ln_g": ln_g, "final_g": final_g,
            "ssm_w_in": ssm_w_in, "ssm_conv_w": ssm_conv_w, "ssm_conv_b": ssm_conv_b,
            "ssm_dt_bias": ssm_dt_bias, "ssm_a_log": ssm_a_log, "ssm_d": ssm_d,
            "ssm_norm_g": ssm_norm_g, "ssm_w_out": ssm_w_out,
            "pool_w_in": pool_w_in, "pool_w_group": pool_w_group,
            "pool_scale": pool_scale, "pool_w_out": pool_w_out}


def reference(x, ln_g, final_g, ssm_w_in, ssm_conv_w, ssm_conv_b, ssm_dt_bias, ssm_a_log,
              ssm_d, ssm_norm_g, ssm_w_out, pool_w_in, pool_w_group, pool_scale, pool_w_out):
    for i in range(DEPTH):
        j = i // N_MIXERS
        hn = rmsnorm(x, ln_g[i])
        if i % N_MIXERS == 0:
            x = x + ssm_mixer(hn, ssm_w_in[j], ssm_conv_w[j], ssm_conv_b[j], ssm_dt_bias[j],
                              ssm_a_log[j], ssm_d[j], ssm_norm_g[j], ssm_w_out[j])
        else:
            x = x + pool_mixer(hn, pool_w_in[j], pool_w_group[j], pool_scale[j], pool_w_out[j])
    return rmsnorm(x, final_g)
```

```python
import numpy as np
from contextlib import ExitStack
import concourse.bass as bass
import concourse.mybir as mybir
from concourse.bass_utils import run_bass_kernel_spmd

F32 = mybir.dt.float32
BF16 = mybir.dt.bfloat16
AF = mybir.ActivationFunctionType
ALU = mybir.AluOpType
ENGS = ("pe", "act", "dve", "pool", "sp")

D = 2048
SEQ = 4096
BATCH = 4
DIN = 4096
NH = 64
NG = 8
NST = 128
CONV_DIM = 6144
SSM_IN = 10304
TT = 512
EPS = 1e-6
NEG = -30000.0


class Buf:
    __slots__ = ("name", "w", "readers", "dsem", "dcnt", "grp", "excl")

    def __init__(self, name):
        self.name = name
        self.w = None
        self.readers = {}
        self.dsem = None
        self.dcnt = None
        self.grp = None
        self.excl = False


class Op:
    __slots__ = ("eng", "fn", "deps", "sig", "is_dma", "need", "waits")

    def __init__(self, eng, fn, is_dma):
        self.eng = eng
        self.fn = fn
        self.deps = []
        self.sig = None
        self.is_dma = is_dma
        self.need = False
        self.waits = []


class Prog:
    def __init__(self, nc, es):
        self.nc = nc
        self.es = es
        self.ops = {e: [] for e in ENGS}
        self.sem = {e: es.enter_context(nc.semaphore("c_" + e)) for e in ("pe", "act", "dve", "pool")}
        self.groups = []
        self.nd = 0

    def new_dsem(self):
        self.nd += 1
        return [self.es.enter_context(self.nc.semaphore("d%d" % self.nd))], [0]

    def share_dsem(self, bufs, group=False):
        s, c = self.new_dsem()
        g = [s, c, []] if group else None
        if group:
            self.groups.append(g)
        for b in bufs:
            b.dsem, b.dcnt = s, c
            b.grp = g

    def handoff(self, olds, news):
        ops = {}
        for b in olds:
            if b.w is not None:
                ops[id(b.w)] = b.w
            for r in b.readers.values():
                ops[id(r)] = r
        for nb in news:
            for k, o in ops.items():
                nb.readers[k] = o

    def add(self, eng, fn, reads=(), writes=(), dma=None):
        op = Op(eng, fn, dma is not None)
        deps = op.deps
        for b in reads:
            if b.w is not None:
                deps.append((b.w, 0))
            if b.excl:
                for k, r in b.readers.items():
                    if k != eng:
                        deps.append((r, 2))
        for b in writes:
            if b.w is not None:
                deps.append((b.w, 1))
            for r in b.readers.values():
                if r is not op:
                    deps.append((r, 2))
        for b in writes:
            b.w = op
            b.readers = {}
        for b in reads:
            if b.w is op:
                continue
            key = eng if dma is None else id(op)
            b.readers[key] = op
        if dma is not None:
            if dma.dsem is None:
                dma.dsem, dma.dcnt = self.new_dsem()
            dma.dcnt[0] += 16
            op.sig = [dma.dsem[0], dma.dcnt[0]]
            if dma.grp is not None:
                dma.grp[2].append(op)
        self.ops[eng].append(op)
        return op

    def emit(self):
        nc = self.nc
        for g in self.groups:
            for op in g[2]:
                op.sig[1] = g[1][0]
        for e in ENGS:
            for op in self.ops[e]:
                for d, kind in op.deps:
                    if d.eng == op.eng and not d.is_dma and not op.is_dma:
                        if e == "pe" or kind != 0:
                            continue
                    d.need = True
                    op.waits.append(d)
        for e in ENGS:
            cnt = 0
            for op in self.ops[e]:
                if op.is_dma:
                    continue
                if op.need:
                    cnt += 1
                    op.sig = [self.sem[e], cnt]

        def run(e, eng):
            waited = {}
            for op in self.ops[e]:
                for d in op.waits:
                    s, v = d.sig
                    k = id(s)
                    if waited.get(k, 0) < v:
                        eng.wait_ge(s, v)
                        waited[k] = v
                if op.fn is None:
                    continue
                ins = op.fn(eng)
                if op.is_dma:
                    ins.then_inc(op.sig[0], 16)
                elif op.need:
                    ins.then_inc(op.sig[0], 1)

        with nc.Block() as block:
            @block.tensor
            def _(eng):
                run("pe", eng)

            @block.scalar
            def _(eng):
                run("act", eng)

            @block.vector
            def _(eng):
                run("dve", eng)

            @block.gpsimd
            def _(eng):
                run("pool", eng)

            @block.sync
            def _(eng):
                run("sp", eng)


def build_program(NPRE=3, NMAIN=5, layers=(0, 1), final=True):
    NT = NPRE + NMAIN
    ntok = NT * TT
    nout = (NMAIN - 1) * TT if NPRE > 0 else NMAIN * TT
    nc = bass.Bass("TRN2", target_bir_lowering=False)
    dr = lambda n, s: nc.dram_tensor(n, s, F32, kind="ExternalInput").ap()
    x_d = dr("x", [ntok, D])
    w_in0 = dr("ssm_w_in", [D, SSM_IN])
    w_out0 = dr("ssm_w_out", [DIN, D])
    w_in1 = dr("pool_w_in", [D, 2 * DIN])
    w_grp = dr("pool_w_group", [DIN, 1024])
    w_out1 = dr("pool_w_out", [DIN, D])
    cmask_d = dr("cmask", [128, 6, 128])
    convw_d = dr("convw", [128, 48, 4])
    convb_d = dr("convb", [128, 48])
    rows_d = dr("rows", [1, 3 * 64])
    lng_d = dr("lng", [128, 2, 16])
    fg_d = dr("fg", [1, D])
    ng_d = dr("ng", [128, 32])
    psc_d = dr("psc", [128, 32])
    invc_d = dr("invc", [1, 64])
    flag_d = dr("flag", [128, 1])
    out_d = nc.dram_tensor("out", [nout, D], F32, kind="ExternalOutput").ap()

    with ExitStack() as es:
        P = Prog(nc, es)
        add = P.add

        def SB(name, shape, dt):
            return es.enter_context(nc.sbuf_tensor("s_" + name, shape, dt))

        def PS(name, shape, dt):
            return es.enter_context(nc.psum_tensor("p_" + name, shape, dt))

        rr = {"i": 0}

        def ev():
            rr["i"] += 1
            return "act" if rr["i"] & 1 else "dve"

        cmask = SB("cmask", [128, 6, 128], F32)
        cmb = SB("cmb", [128, 6, 128], BF16)
        convw = SB("convw", [128, 48, 4], F32)
        convb = SB("convb", [128, 48], F32)
        rows = SB("rows", [128, 3, 64], F32)
        arow = SB("arow", [128, 64], F32)
        lng = SB("lng", [128, 2, 16], F32)
        fgrow = SB("fgrow", [128, D], F32)
        ngf = SB("ngf", [128, 32], F32)
        pscf = SB("pscf", [128, 32], F32)
        invc = SB("invc", [128, 4, 16], F32)
        wdt = SB("wdt", [128, 16, 64], BF16)
        flag = SB("flag", [128, 1], F32)
        b_const = [Buf("c%d" % i) for i in range(11)]
        P.share_dsem(b_const, group=True)
        (b_cmask, b_convw, b_convb, b_rows, b_lng, b_fg, b_ng, b_psc, b_invc, b_wdt, b_flag) = b_const
        add("sp", lambda e: e.dma_start(out=flag[:], in_=flag_d), writes=[b_flag], dma=b_flag)
        add("sp", lambda e: e.dma_start(out=cmask[:], in_=cmask_d), writes=[b_cmask], dma=b_cmask)
        add("sp", lambda e: e.dma_start(out=convw[:], in_=convw_d), writes=[b_convw], dma=b_convw)
        add("sp", lambda e: e.dma_start(out=convb[:], in_=convb_d), writes=[b_convb], dma=b_convb)
        add("sp", lambda e: e.dma_start(out=rows[:].rearrange("p a b -> p (a b)"), in_=rows_d.partition_broadcast(128)),
            writes=[b_rows], dma=b_rows)
        add("sp", lambda e: e.dma_start(out=lng[:], in_=lng_d), writes=[b_lng], dma=b_lng)
        add("sp", lambda e: e.dma_start(out=fgrow[:], in_=fg_d.partition_broadcast(128)), writes=[b_fg], dma=b_fg)
        add("sp", lambda e: e.dma_start(out=ngf[:], in_=ng_d), writes=[b_ng], dma=b_ng)
        add("sp", lambda e: e.dma_start(out=pscf[:], in_=psc_d), writes=[b_psc], dma=b_psc)
        add("sp", lambda e: e.dma_start(out=invc[:].rearrange("p a b -> p (a b)"), in_=invc_d.partition_broadcast(128)),
            writes=[b_invc], dma=b_invc)
        add("pool", lambda e: e.dma_start(out=wdt[:], in_=w_in0[:, 10240:10304].rearrange("(kc p) n -> p kc n", p=128)),
            writes=[b_wdt], dma=b_wdt)
        b_cmb = Buf("cmb")
        add("dve", lambda e: e.tensor_copy(out=cmb[:], in_=cmask[:]), reads=[b_cmask], writes=[b_cmb])
        b_arow = Buf("arow")
        add("act", lambda e: e.activation(out=arow[:], in_=rows[:, 1, :], func=AF.Exp), reads=[b_rows], writes=[b_arow])
        add("dve", lambda e: e.tensor_scalar(out=arow[:], in0=arow[:], scalar1=-1.0, scalar2=None, op0=ALU.mult),
            reads=[b_arow], writes=[b_arow])
        IDENT, ONES, NTRI, TRILE, TRIGT, NMASK = range(6)

        xres = SB("xres", [128, 4, D], F32)
        b_xres = [Buf("xres%d" % i) for i in range(4)]
        hnT = SB("hnT", [128, 16, TT], BF16)
        b_hnT = [Buf("hnT%d" % i) for i in range(4)]
        hnst = SB("hnst", [128, D], BF16)
        b_hnst = Buf("hnst")
        small = SB("small", [128, 64], F32)
        b_small = [Buf("small%d" % i) for i in range(64)]
        S = SB("S", [128, NG, 512], F32)
        b_S = [Buf("S%d" % g) for g in range(NG)]
        Sbf = SB("Sbf", [128, 2, 512], BF16)
        b_Sbf = [Buf("Sbf%d" % g) for g in range(2)]
        ygT = SB("ygT", [128, 32, TT], BF16)
        b_ygT = [Buf("ygT%d" % k) for k in range(32)]
        ring = SB("ring", [128, 4, 4096], BF16)
        b_ring = [Buf("ring%d" % i) for i in range(4)]
        chalo = SB("chalo", [128, 48, 3], F32)
        b_chalo = [Buf("chalo%d" % i) for i in range(48)]
        uhalo = SB("uhalo", [128, 32, 15], F32)
        b_uhalo = [Buf("uhalo%d" % i) for i in range(32)]
        ssq = SB("ssq", [128, 4, 8], F32)
        b_ssq = [Buf("ssq%d" % i) for i in range(4)]
        dtf = SB("dtf", [128, 4, 5, 64], F32)
        adtb = SB("adtb", [128, 4, 64], BF16)
        b_dt = [[Buf("dt%d_%d" % (t, j)) for j in range(6)] for t in range(4)]
        dtmp = SB("dtmp", [128, 3, 64], F32)
        b_dtmp = [Buf("dtmp%d" % j) for j in range(3)]

        SCR_BYTES = 38 * 1024
        scr = SB("scr", [128, SCR_BYTES // 2], BF16)
        off = {"v": 0}

        def carve(nelem, dt, shape=None):
            nb = nelem * (4 if dt == F32 else 2)
            o = off["v"]
            off["v"] = o + ((nb + 31) // 32) * 32
            assert off["v"] <= SCR_BYTES, off["v"]
            ap = scr[:, o // 2:(o + nb) // 2]
            if dt == F32:
                ap = ap.bitcast(F32)
            return ap

        raw = [carve(515, F32) for _ in range(2)]
        acc = [carve(512, F32) for _ in range(2)]
        BT2 = [carve(512, BF16) for _ in range(2)]
        CT2 = [carve(512, BF16) for _ in range(2)]
        Btok = carve(512, BF16)
        sz2 = [carve(2048, BF16) for _ in range(2)]
        xdt = [carve(512, BF16)] * 2
        xD = [carve(512, BF16)] * 2
        xw = [carve(512, BF16)] * 2
        rhs1 = [carve(1024, BF16)] * 2
        decT = [carve(1024, BF16)] * 2
        MT = [carve(1024, BF16)] * 2
        yoffb = carve(512, BF16)
        yg = [carve(512, BF16) for _ in range(2)]
        sqj = carve(512, BF16)
        l0_end = off["v"]
        b_raw = [Buf("raw%d" % i) for i in range(2)]
        b_acc = [Buf("acc%d" % i) for i in range(2)]
        b_BT2, b_CT2 = [Buf("BT0"), Buf("BT1")], [Buf("CT0"), Buf("CT1")]
        b_Btok = [Buf("Btok%d" % i) for i in range(4)]
        b_sz2 = [[Buf("sz%d_%d" % (p_, i)) for i in range(4)] for p_ in range(2)]
        b_xdt = [Buf("xdt")] * 2
        b_xD = [Buf("xD")] * 2
        b_xw = [Buf("xw")] * 2
        b_rhs1 = [Buf("rhs1")] * 2
        b_decT = [Buf("decT")] * 2
        b_MT = [Buf("MT")] * 2
        b_yoffb = Buf("yoffb")
        b_yg = [Buf("yg%d" % i) for i in range(2)]
        b_sqj = Buf("sqj")
        L0_BUFS = (b_raw + b_acc + b_BT2 + b_CT2 + b_Btok + b_sz2[0] + b_sz2[1] + b_xdt[:1] + b_xD[:1] + b_xw[:1] + b_rhs1[:1] +
                   b_decT[:1] + b_MT[:1] + [b_yoffb] + b_yg + [b_sqj])
        off["v"] = 0
        uraw = [carve(527, F32) for _ in range(2)]
        s2 = carve(527, F32)
        s4 = carve(527, F32)
        s8 = carve(527, F32)
        mixT = carve(8 * 512, BF16)
        sgT = carve(8 * 512, BF16)
        utmp = carve(16, F32)
        b_uraw = [Buf("uraw%d" % i) for i in range(2)]
        b_s2, b_s4, b_s8, b_utmp = Buf("s2"), Buf("s4"), Buf("s8"), Buf("utmp")
        b_mixT = [Buf("mixT%d" % i) for i in range(8)]
        b_sgT = [Buf("sgT%d" % i) for i in range(8)]
        L1_BUFS = b_uraw + [b_s2, b_s4, b_s8, b_utmp] + b_mixT + b_sgT

        pA = [PS("pA%d" % i, [128, 512], F32) for i in range(2)]
        b_pA = [Buf("pA%d" % i) for i in range(2)]
        pT = PS("pT", [128, 1024], BF16)
        b_pT = [Buf("pT")] * 2
        pC = PS("pC", [128, 512], F32)
        b_pC = [Buf("pC")] * 4
        pSeg = PS("pSeg", [128, 1024], F32)
        b_pSeg = Buf("pSeg")
        pY = PS("pY", [128, 512], F32)
        b_pY = Buf("pY")
        pO = PS("pO", [128, 512], F32)
        b_pO = Buf("pO")
        acc_banks = [(pA[0], b_pA[0]), (pA[1], b_pA[1]), (pY, b_pY), (pO, b_pO)]
        for _b in b_pA + b_pT + b_pC + [b_pSeg, b_pY, b_pO]:
            _b.excl = True
        tr_banks = [(pT[:, 0:512], b_pT[0]), (pY[:, 0:256].bitcast(BF16), b_pY),
                    (pO[:, 0:256].bitcast(BF16), b_pO), (pSeg[:, 0:256].bitcast(BF16), b_pSeg)]

        b_init = Buf("init")
        add("dve", lambda e: e.memset(S[:], 0.0), writes=b_S)
        add("dve", lambda e: e.memset(Sbf[:], 0.0), writes=b_Sbf)
        add("dve", lambda e: e.memset(chalo[:], 0.0), writes=b_chalo)
        add("dve", lambda e: e.memset(uhalo[:], 0.0), writes=b_uhalo)

        ringst = {"i": 0}

        def load_slab(src_ap, nkc, ncols):
            i = ringst["i"] % 4
            ringst["i"] += 1
            view = ring[:, i, 0:nkc * ncols].rearrange("p (k n) -> p k n", k=nkc)
            src = src_ap.rearrange("(k p) n -> p k n", p=128)
            add("pool", lambda e: e.dma_start(out=view, in_=src), writes=[b_ring[i]], dma=b_ring[i])
            return view, b_ring[i]

        def mm_fn(out_ap, pairs, first=True, last=True):
            def fn(e):
                n = len(pairs)
                ins = None
                for j, (l, r) in enumerate(pairs):
                    ins = e.matmul(out_ap, lhsT=l, rhs=r, start=(first and j == 0), stop=(last and j == n - 1))
                return ins
            return fn

        def act_fn(out, in_, func, **kw):
            return lambda e: e.activation(out=out, in_=in_, func=func, **kw)

        def copy_on(eng, out, in_, reads, writes):
            if eng == "act":
                add("act", lambda e: e.activation(out=out, in_=in_, func=AF.Copy), reads=reads, writes=writes)
            else:
                add(eng, lambda e: e.tensor_copy(out=out, in_=in_), reads=reads, writes=writes)

        def tt_fn(out, in0, in1, op):
            return lambda e: e.tensor_tensor(out=out, in0=in0, in1=in1, op=op)

        def ts_fn(out, in0, s1, s2, op0, op1=None, **kw):
            if op1 is None:
                return lambda e: e.tensor_scalar(out=out, in0=in0, scalar1=s1, scalar2=None, op0=op0, **kw)
            return lambda e: e.tensor_scalar(out=out, in0=in0, scalar1=s1, scalar2=s2, op0=op0, op1=op1, **kw)

        def stt_fn(out, in0, scalar, in1, op0, op1, **kw):
            return lambda e: e.scalar_tensor_tensor(out=out, in0=in0, scalar=scalar, in1=in1, op0=op0, op1=op1, **kw)

        def norm_to_hnT(layer):
            for tb in range(4):
                ss = small[:, tb * 4:tb * 4 + 1]
                rs = small[:, tb * 4 + 1:tb * 4 + 2]
                b_ss, b_rs = b_small[tb * 4], b_small[tb * 4 + 1]
                add("dve", stt_fn(hnst[:], xres[:, tb, :], 1.0, xres[:, tb, :], ALU.mult, ALU.mult, accum_out=ss),
                    reads=[b_xres[tb]], writes=[b_hnst, b_ss])
                add("act", act_fn(rs, ss, AF.Sqrt, scale=1.0 / D, bias=epsb[:, 0:1]), reads=[b_ss, b_eps], writes=[b_rs])
                add("dve", lambda e, rs=rs: e.reciprocal(out=rs, in_=rs), reads=[b_rs], writes=[b_rs])
                add("act", act_fn(hnst[:], xres[:, tb, :], AF.Copy, scale=rs), reads=[b_xres[tb], b_rs], writes=[b_hnst])
                for rnd in range(4):
                    pt, b_pt = tr_banks[rnd]

                    def trf(e, rnd=rnd, pt=pt):
                        ins = None
                        for j in range(4):
                            kc = rnd * 4 + j
                            ins = e.transpose(out=pt[:, j * 128:(j + 1) * 128], in_=hnst[:, kc * 128:(kc + 1) * 128],
                                              identity=cmb[:, IDENT, :])
                        return ins
                    add("pe", trf, reads=[b_hnst, b_cmb], writes=[b_pt])
                    o = hnT[:, rnd * 4:rnd * 4 + 4, tb * 128:(tb + 1) * 128]
                    g_bc = lng[:, layer, rnd * 4:rnd * 4 + 4].unsqueeze(2).to_broadcast([128, 4, 128])
                    add("dve", tt_fn(o, pt.rearrange("p (a b) -> p a b", a=4), g_bc, ALU.mult),
                        reads=[b_pt, b_lng], writes=[b_hnT[tb]])

        epsb = SB("epsb", [128, 1], F32)
        b_eps = Buf("eps")
        oneb = SB("oneb", [128, 1], F32)

        def _mk_eps(e):
            e.memset(oneb[:], 1.0)
            return e.memset(epsb[:], EPS)
        add("dve", _mk_eps, writes=[b_eps])

        def out_proj(w_ap, rstd_aps, rstd_bufs):
            for cb in range(4):
                for s in range(4):
                    view, rb = load_slab(w_ap[s * 1024:(s + 1) * 1024, cb * 512:(cb + 1) * 512], 8, 512)
                    for tb in range(4):
                        pt_, bpt = acc_banks[tb]
                        pairs = [(ygT[:, s * 8 + k, tb * 128:(tb + 1) * 128], view[:, k, :]) for k in range(8)]
                        add("pe", mm_fn(pt_[:], pairs, first=(s == 0), last=(s == 3)),
                            reads=[rb] + b_ygT[s * 8:s * 8 + 8], writes=[bpt])
                for tb in range(4):
                    pt_, bpt = acc_banks[tb]
                    xr = xres[:, tb, cb * 512:(cb + 1) * 512]
                    if rstd_aps is None:
                        add("dve", tt_fn(xr, pt_[:], xr, ALU.add), reads=[bpt, b_xres[tb]], writes=[b_xres[tb]])
                    else:
                        add("dve", stt_fn(xr, pt_[:], rstd_aps[tb], xr, ALU.mult, ALU.add),
                            reads=[bpt, b_xres[tb], rstd_bufs[tb]], writes=[b_xres[tb]])

        def layer0(mode):
            pre = (mode == 'pre')
            norm_to_hnT(0)
            for tb in range(4):
                DT, ADT, ETOK, WEXP, EL = range(5)
                bd = b_dt[tb]
                pairs = [(hnT[:, kc, tb * 128:(tb + 1) * 128], wdt[:, kc, :]) for kc in range(16)]
                add("pe", mm_fn(pA[0][:, 0:64], pairs), reads=[b_hnT[tb], b_wdt], writes=[b_pA[0]])
                v = dtmp[:, 0, :]
                a = dtmp[:, 1, :]
                l = dtmp[:, 2, :]
                add("dve", tt_fn(v, pA[0][:, 0:64], rows[:, 0, :], ALU.add), reads=[b_pA[0], b_rows], writes=[b_dtmp[0]])
                add("dve", stt_fn(a, v, -1.0, v, ALU.mult, ALU.max), reads=[b_dtmp[0]], writes=[b_dtmp[1]])
                add("act", act_fn(a, a, AF.Exp, scale=-1.0), reads=[b_dtmp[1]], writes=[b_dtmp[1]])
                add("act", act_fn(l, a, AF.Ln, bias=oneb[:, 0:1]), reads=[b_dtmp[1], b_eps], writes=[b_dtmp[2]])
                add("dve", stt_fn(dtf[:, tb, DT, :], v, 0.0, l, ALU.max, ALU.add), reads=[b_dtmp[0], b_dtmp[2]], writes=[bd[DT]])
                add("dve", tt_fn(dtf[:, tb, ADT, :], dtf[:, tb, DT, :], arow[:], ALU.mult), reads=[bd[DT], b_arow], writes=[bd[ADT]])
                copy_on("dve", adtb[:, tb, :], dtf[:, tb, ADT, :], [bd[ADT]], [bd[5]])
                for j, (msk, dst) in enumerate(((TRILE, ETOK), (TRIGT, WEXP), (ONES, EL))):
                    pc = pC[:, 128 + j * 64:128 + (j + 1) * 64]
                    add("pe", mm_fn(pc, [(cmask[:, msk, :], dtf[:, tb, ADT, :])]), reads=[b_cmask, bd[ADT]], writes=[b_pC[1 + j]])
                    add("act", act_fn(dtf[:, tb, dst, :], pc, AF.Exp), reads=[b_pC[1 + j]], writes=[bd[dst]])
            for tb in range(4):
                add("dve", lambda e, tb=tb: e.memset(ssq[:, tb, :], 0.0), writes=[b_ssq[tb]])

            blk = {"i": 0}

            def conv_block(ps_ap, b_ps, cbi, out_ap, b_out):
                i = blk["i"] & 1
                blk["i"] += 1
                r, a_ = raw[i], acc[i]
                add("dve", lambda e: e.tensor_copy(out=r[:, 0:3], in_=chalo[:, cbi, :]), reads=[b_chalo[cbi]], writes=[b_raw[i]])
                add("act", act_fn(r[:, 3:515], ps_ap, AF.Copy), reads=[b_ps], writes=[b_raw[i]])
                add("dve", lambda e: e.tensor_copy(out=chalo[:, cbi, :], in_=r[:, 512:515]), reads=[b_raw[i]], writes=[b_chalo[cbi]])
                add("dve", ts_fn(a_, r[:, 0:512], convw[:, cbi, 0:1], None, ALU.mult), reads=[b_raw[i], b_convw], writes=[b_acc[i]])
                for k in range(1, 4):
                    add("dve", stt_fn(a_, r[:, k:k + 512], convw[:, cbi, k:k + 1], a_, ALU.mult, ALU.add),
                        reads=[b_raw[i], b_convw, b_acc[i]], writes=[b_acc[i]])
                add("act", act_fn(out_ap, a_, AF.Silu, bias=convb[:, cbi:cbi + 1]), reads=[b_acc[i], b_convb], writes=[b_out])

            pa = {"i": 0}

            def next_pa():
                i = pa["i"] & 1
                pa["i"] += 1
                return pA[i], b_pA[i]

            def A_thunks(g):
                par = g & 1
                th = []
                if not pre:
                    stz = {}

                    def z_thunk(tb, stz=stz):
                        if tb == 0:
                            stz["a"] = load_slab(w_in0[0:1024, g * 512:(g + 1) * 512], 8, 512)
                            stz["b"] = load_slab(w_in0[1024:2048, g * 512:(g + 1) * 512], 8, 512)
                        (v0, r0), (v1, r1) = stz["a"], stz["b"]
                        pp, bpp = next_pa()
                        pairs = [(hnT[:, kc, tb * 128:(tb + 1) * 128], (v0 if kc < 8 else v1)[:, kc % 8, :]) for kc in range(16)]
                        add("pe", mm_fn(pp[:], pairs), reads=[b_hnT[tb], r0, r1], writes=[bpp])
                        add("act", act_fn(sz2[par][:, tb * 512:(tb + 1) * 512], pp[:], AF.Silu), reads=[bpp], writes=[b_sz2[par][tb]])
                    for tb in range(4):
                        th.append(lambda tb=tb: z_thunk(tb))
                stx = {}

                def x_thunk(c, stx=stx):
                    c0 = 4096 + g * 512
                    if c == 0:
                        stx["a"] = load_slab(w_in0[0:1024, c0:c0 + 512], 8, 512)
                        stx["b"] = load_slab(w_in0[1024:2048, c0:c0 + 512], 8, 512)
                    (v0, r0), (v1, r1) = stx["a"], stx["b"]
                    pp, bpp = next_pa()
                    pairs = [((v0 if kc < 8 else v1)[:, kc % 8, c * 128:(c + 1) * 128], hnT[:, kc, :]) for kc in range(16)]
                    add("pe", mm_fn(pp[:], pairs), reads=b_hnT + [r0, r1], writes=[bpp])
                    conv_block(pp[:], bpp, g * 4 + c, xact42[par][c], b_xact42[par][c])
                for c in range(4):
                    th.append(lambda c=c: x_thunk(c))

                def bc_thunk(which):
                    col = (8192 if which == 0 else 9216) + g * 128
                    vB, rB = load_slab(w_in0[:, col:col + 128], 16, 128)
                    pp, bpp = next_pa()
                    add("pe", mm_fn(pp[:], [(vB[:, kc, :], hnT[:, kc, :]) for kc in range(16)]), reads=b_hnT + [rB], writes=[bpp])
                    if which == 0:
                        conv_block(pp[:], bpp, 32 + g, BT2[par], b_BT2[par])
                    else:
                        conv_block(pp[:], bpp, 40 + g, CT2[par], b_CT2[par])
                th.append(lambda: bc_thunk(0))
                th.append(lambda: bc_thunk(1))
                return th

            pending = []

            def pump():
                if pending:
                    pending.pop(0)()

            for th_ in A_thunks(0):
                th_()
            for g in range(NG):
                par = g & 1
                xact4, b_xact4 = xact42[par], b_xact42[par]
                BT, CT, b_BT, b_CT = BT2[par], CT2[par], b_BT2[par], b_CT2[par]
                sz, b_sz = sz2[par], b_sz2[par]
                pending.extend(A_thunks(g + 1) if g + 1 < NG else [])
                if not pre:
                    copy_on("act", Sbf[:, g & 1, :], S[:, g, :], [b_S[g]], [b_Sbf[g & 1]])
                for tb in range(4):
                    DT, ADT, ETOK, WEXP, EL = range(5)
                    bd = b_dt[tb]
                    i2 = tb & 1
                    tsl = slice(tb * 128, (tb + 1) * 128)
                    hs = slice(g * 8, (g + 1) * 8)
                    ph = pT[:, 0:512]
                    pb = pT[:, 512:640]

                    def trx(e, tsl=tsl, ph=ph, pb=pb, xact4=xact4, BT=BT):
                        for c in range(4):
                            e.transpose(out=ph[:, c * 128:(c + 1) * 128], in_=xact4[c][:, tsl], identity=cmb[:, IDENT, :])
                        return e.transpose(out=pb, in_=BT[:, tsl], identity=cmb[:, IDENT, :])
                    add("pe", trx, reads=b_xact4 + [b_BT, b_cmb], writes=[b_pT[0]])
                    pump()
                    ph3 = ph.rearrange("p (h q) -> p h q", h=8)
                    dt_bc = dtf[:, tb, DT, hs].unsqueeze(2).to_broadcast([128, 8, 64])
                    d_bc = rows[:, 2, hs].unsqueeze(2).to_broadcast([128, 8, 64])
                    add("dve", tt_fn(xdt[i2].rearrange("p (h q) -> p h q", h=8), ph3, dt_bc, ALU.mult),
                        reads=[b_pT[0], bd[DT]], writes=[b_xdt[i2]])
                    if not pre:
                        add("dve", tt_fn(xD[i2].rearrange("p (h q) -> p h q", h=8), ph3, d_bc, ALU.mult),
                            reads=[b_pT[0], b_rows], writes=[b_xD[i2]])
                    add("dve", lambda e, tb=tb, pb=pb: e.tensor_copy(out=Btok[:, tb * 128:(tb + 1) * 128], in_=pb),
                        reads=[b_pT[0]], writes=[b_Btok[tb]])
                    if not pre:
                        add("pe", mm_fn(pC[:, 0:128], [(BT[:, tsl], CT[:, tsl])]), reads=[b_BT, b_CT], writes=[b_pC[0]])
                        r1v = rhs1[i2].rearrange("p (h l) -> p h l", h=8)
                        add("dve", tt_fn(r1v, cmb[:, TRILE, :].unsqueeze(1).to_broadcast([128, 8, 128]),
                                         dtf[:, tb, ADT, hs].unsqueeze(2).to_broadcast([128, 8, 128]), ALU.mult),
                            reads=[b_cmb, bd[ADT]], writes=[b_rhs1[i2]])

                        def segf(e, tb=tb, r1v=r1v, g=g):
                            ins = None
                            for hh in range(2):
                                o = pSeg[:, hh * 512:(hh + 1) * 512].rearrange("p (h l) -> p h l", h=4)
                                e.matmul(o, lhsT=cmb[:, ONES, :], rhs=r1v[:, hh * 4:(hh + 1) * 4, :], start=True, stop=False)
                                e.matmul(o, lhsT=cmb[:, NTRI, :],
                                         rhs=adtb[:, tb, g * 8 + hh * 4:g * 8 + hh * 4 + 4].unsqueeze(2).to_broadcast([128, 4, 128]),
                                         start=False, stop=False)
                                ins = e.matmul(o, lhsT=cmb[:, IDENT, :],
                                               rhs=cmb[:, NMASK, :].unsqueeze(1).to_broadcast([128, 4, 128]), start=False, stop=True)
                            return ins
                        add("pe", segf, reads=[b_cmb, b_rhs1[i2], bd[5]], writes=[b_pSeg])
                        pump()
                        add("act", act_fn(decT[i2], pSeg[:], AF.Exp), reads=[b_pSeg], writes=[b_decT[i2]])
                        mtv = MT[i2].rearrange("p (h l) -> p h l", h=8)
                        add("dve", tt_fn(mtv, decT[i2].rearrange("p (h l) -> p h l", h=8),
                                         pC[:, 0:128].unsqueeze(1).to_broadcast([128, 8, 128]), ALU.mult),
                            reads=[b_decT[i2], b_pC[0]], writes=[b_MT[i2]])
                        add("pe", mm_fn(pO[:], [(CT[:, tsl], Sbf[:, g & 1, :])]), reads=[b_CT, b_Sbf[g & 1]], writes=[b_pO])
                        add("dve", tt_fn(yoffb.rearrange("p (h q) -> p h q", h=8), pO[:].rearrange("p (h q) -> p h q", h=8),
                                         dtf[:, tb, ETOK, hs].unsqueeze(2).to_broadcast([128, 8, 64]), ALU.mult),
                            reads=[b_pO, bd[ETOK]], writes=[b_yoffb])

                        def yf(e, i2=i2, mtv=mtv):
                            e.matmul(pY[:], lhsT=cmb[:, IDENT, :], rhs=xD[i2], start=True, stop=False)
                            for h in range(8):
                                e.matmul(pY[:, h * 64:(h + 1) * 64], lhsT=mtv[:, h, :], rhs=xdt[i2][:, h * 64:(h + 1) * 64],
                                         start=False, stop=False)
                            return e.matmul(pY[:], lhsT=cmb[:, IDENT, :], rhs=yoffb, start=False, stop=True)
                        add("pe", yf, reads=[b_cmb, b_xD[i2], b_xdt[i2], b_MT[i2], b_yoffb], writes=[b_pY])
                        pump()
                        add("dve", tt_fn(yg[i2], pY[:], sz[:, tb * 512:(tb + 1) * 512], ALU.mult), reads=[b_pY, b_sz[tb]], writes=[b_yg[i2]])
                        add("act", act_fn(sqj, yg[i2], AF.Square, accum_out=ssq[:, tb, g:g + 1]), reads=[b_yg[i2]], writes=[b_sqj, b_ssq[tb]])
                        ph = pT[:, 0:512]

                        def try_(e, i2=i2, ph=ph):
                            ins = None
                            for c in range(4):
                                ins = e.transpose(out=ph[:, c * 128:(c + 1) * 128], in_=yg[i2][:, c * 128:(c + 1) * 128],
                                                  identity=cmb[:, IDENT, :])
                            return ins
                        add("pe", try_, reads=[b_yg[i2], b_cmb], writes=[b_pT[0]])
                        add("dve", tt_fn(ygT[:, g * 4:g * 4 + 4, tsl], ph.rearrange("p (a b) -> p a b", a=4),
                                         ngf[:, g * 4:g * 4 + 4].unsqueeze(2).to_broadcast([128, 4, 128]), ALU.mult),
                            reads=[b_pT[0], b_ng], writes=b_ygT[g * 4:g * 4 + 4])
                    add("dve", tt_fn(xw[i2].rearrange("p (h q) -> p h q", h=8), xdt[i2].rearrange("p (h q) -> p h q", h=8),
                                     dtf[:, tb, WEXP, hs].unsqueeze(2).to_broadcast([128, 8, 64]), ALU.mult),
                        reads=[b_xdt[i2], bd[WEXP]], writes=[b_xw[i2]])
                    add("pe", mm_fn(pO[:], [(Btok[:, tb * 128:(tb + 1) * 128], xw[i2])]), reads=[b_Btok[tb], b_xw[i2]], writes=[b_pO])
                    if pre:
                        pump()
                    sv = S[:, g, :].rearrange("p (h q) -> p h q", h=8)
                    add("dve", tt_fn(sv, sv, dtf[:, tb, EL, hs].unsqueeze(2).to_broadcast([128, 8, 64]), ALU.mult),
                        reads=[b_S[g], bd[EL]], writes=[b_S[g]])
                    add("dve", tt_fn(S[:, g, :], S[:, g, :], pO[:], ALU.add), reads=[b_S[g], b_pO], writes=[b_S[g]])
                    if not pre:
                        copy_on("act", Sbf[:, g & 1, :], S[:, g, :], [b_S[g]], [b_Sbf[g & 1]])
                while pending:
                    pump()
            if pre:
                return
            rst, rbs = [], []
            for tb in range(4):
                sst = small[:, 32 + tb * 2:33 + tb * 2]
                rs = small[:, 33 + tb * 2:34 + tb * 2]
                b_sst, b_rs = b_small[32 + tb * 2], b_small[33 + tb * 2]
                add("dve", lambda e, sst=sst, tb=tb: e.tensor_reduce(out=sst, in_=ssq[:, tb, :], axis=mybir.AxisListType.X, op=ALU.add),
                    reads=[b_ssq[tb]], writes=[b_sst])
                add("act", act_fn(rs, sst, AF.Sqrt, scale=1.0 / DIN, bias=epsb[:, 0:1]), reads=[b_sst, b_eps], writes=[b_rs])
                add("dve", lambda e, rs=rs: e.reciprocal(out=rs, in_=rs), reads=[b_rs], writes=[b_rs])
                rst.append(rs)
                rbs.append(b_rs)
            out_proj(w_out0, rst, rbs)


        xact_t = SB("xact4", [128, 2, 4, 512], BF16)
        xact42 = [[xact_t[:, p_, c, :] for c in range(4)] for p_ in range(2)]
        b_xact42 = [[Buf("xact4_%d_%d" % (p_, c)) for c in range(4)] for p_ in range(2)]

        def layer1(mode, first_tile):
            halo_only = (mode == 'm0')
            norm_to_hnT(1)
            ub = {"i": 0}
            for pg in range(4):
                win = 2 << pg
                for j in range(2):
                    c0 = pg * 1024 + j * 512
                    v0, r0 = load_slab(w_in1[0:1024, c0:c0 + 512], 8, 512)
                    v1, r1 = load_slab(w_in1[1024:2048, c0:c0 + 512], 8, 512)
                    for c in range(4):
                        cblk = j * 4 + c
                        ubi = pg * 8 + cblk
                        i = ub["i"] & 1
                        ub["i"] += 1
                        pp, bpp = pA[i], b_pA[i]
                        pairs = [((v0 if kc < 8 else v1)[:, kc % 8, c * 128:(c + 1) * 128], hnT[:, kc, :]) for kc in range(16)]
                        add("pe", mm_fn(pp[:], pairs), reads=b_hnT + [r0, r1], writes=[bpp])
                        u = uraw[i]
                        add("dve", lambda e, u=u, ubi=ubi: e.tensor_copy(out=u[:, 0:15], in_=uhalo[:, ubi, :]),
                            reads=[b_uhalo[ubi]], writes=[b_uraw[i]])
                        add("act", act_fn(u[:, 15:527], pp[:], AF.Copy), reads=[bpp], writes=[b_uraw[i]])
                        add("dve", lambda e, u=u, ubi=ubi: e.tensor_copy(out=uhalo[:, ubi, :], in_=u[:, 512:527]),
                            reads=[b_uraw[i]], writes=[b_uhalo[ubi]])
                        if halo_only:
                            continue
                        lo2 = 15 - (win - 2)
                        add("dve", tt_fn(s2[:, lo2:527], u[:, lo2:527], u[:, lo2 - 1:526], ALU.add), reads=[b_uraw[i]], writes=[b_s2])
                        cur, bcur = s2, b_s2
                        if win >= 4:
                            lo4 = 15 - (win - 4)
                            add("dve", tt_fn(s4[:, lo4:527], s2[:, lo4:527], s2[:, lo4 - 2:525], ALU.add), reads=[b_s2], writes=[b_s4])
                            cur, bcur = s4, b_s4
                        if win >= 8:
                            lo8 = 15 - (win - 8)
                            add("dve", tt_fn(s8[:, lo8:527], s4[:, lo8:527], s4[:, lo8 - 4:523], ALU.add), reads=[b_s4], writes=[b_s8])
                            cur, bcur = s8, b_s8
                        if win >= 16:
                            add("dve", tt_fn(s2[:, 15:527], s8[:, 15:527], s8[:, 7:519], ALU.add), reads=[b_s8], writes=[b_s2])
                            cur, bcur = s2, b_s2
                        mo = mixT[:, cblk * 512:(cblk + 1) * 512]
                        add("dve", stt_fn(mo, cur[:, 15:527], 1.0 / win, u[:, 15:527], ALU.mult, ALU.subtract),
                            reads=[bcur, b_uraw[i]], writes=[b_mixT[cblk]])
                        if first_tile:
                            add("dve", tt_fn(utmp, cur[:, 15:31], invc[:, pg, :], ALU.mult), reads=[bcur, b_invc], writes=[b_utmp])
                            add("dve", tt_fn(mo[:, 0:16], utmp, u[:, 15:31], ALU.subtract),
                                reads=[b_utmp, b_uraw[i], b_mixT[cblk]], writes=[b_mixT[cblk]])
                if halo_only:
                    continue
                for j in range(2):
                    c0 = 4096 + pg * 1024 + j * 512
                    v0, r0 = load_slab(w_in1[0:1024, c0:c0 + 512], 8, 512)
                    v1, r1 = load_slab(w_in1[1024:2048, c0:c0 + 512], 8, 512)
                    for c in range(4):
                        cblk = j * 4 + c
                        i = ub["i"] & 1
                        ub["i"] += 1
                        pp, bpp = pA[i], b_pA[i]
                        pairs = [((v0 if kc < 8 else v1)[:, kc % 8, c * 128:(c + 1) * 128], hnT[:, kc, :]) for kc in range(16)]
                        add("pe", mm_fn(pp[:], pairs), reads=b_hnT + [r0, r1], writes=[bpp])
                        add("act", act_fn(sgT[:, cblk * 512:(cblk + 1) * 512], pp[:], AF.Silu), reads=[bpp], writes=[b_sgT[cblk]])
                for j in range(2):
                    vg, rg = load_slab(w_grp[pg * 1024:(pg + 1) * 1024, j * 512:(j + 1) * 512], 8, 512)
                    for c in range(4):
                        dblk = j * 4 + c
                        i = ub["i"] & 1
                        ub["i"] += 1
                        pp, bpp = pA[i], b_pA[i]
                        pairs = [(vg[:, kc, c * 128:(c + 1) * 128], mixT[:, kc * 512:(kc + 1) * 512]) for kc in range(8)]
                        add("pe", mm_fn(pp[:], pairs), reads=b_mixT + [rg], writes=[bpp])
                        kk = pg * 8 + dblk
                        add("dve", stt_fn(ygT[:, kk, :], pp[:], pscf[:, kk:kk + 1], sgT[:, dblk * 512:(dblk + 1) * 512], ALU.mult, ALU.mult),
                            reads=[bpp, b_psc, b_sgT[dblk]], writes=[b_ygT[kk]])
            if not halo_only:
                out_proj(w_out1, None, None)

        def final_store(t):
            to = t - (NPRE + 1 if NPRE > 0 else 0)
            for tb in range(4):
                ss = small[:, 48 + tb * 2:49 + tb * 2]
                rs = small[:, 49 + tb * 2:50 + tb * 2]
                b_ss, b_rs = b_small[48 + tb * 2], b_small[49 + tb * 2]
                if final:
                    add("dve", stt_fn(hnst[:], xres[:, tb, :], 1.0, xres[:, tb, :], ALU.mult, ALU.mult, accum_out=ss),
                        reads=[b_xres[tb]], writes=[b_hnst, b_ss])
                    add("act", act_fn(rs, ss, AF.Sqrt, scale=1.0 / D, bias=epsb[:, 0:1]), reads=[b_ss, b_eps], writes=[b_rs])
                    add("dve", lambda e, rs=rs: e.reciprocal(out=rs, in_=rs), reads=[b_rs], writes=[b_rs])
                    add("dve", stt_fn(xres[:, tb, :], xres[:, tb, :], rs, fgrow[:], ALU.mult, ALU.mult),
                        reads=[b_xres[tb], b_rs, b_fg], writes=[b_xres[tb]])
                r0 = to * TT + tb * 128
                add("sp", lambda e, tb=tb, r0=r0: e.dma_start(out=out_d[r0:r0 + 128, :], in_=xres[:, tb, :]),
                    reads=[b_xres[tb]], dma=b_xres[tb])

        for t in range(NT):
            for tb in range(4):
                r0 = t * TT + tb * 128
                add("sp", lambda e, tb=tb, r0=r0: e.dma_start(out=xres[:, tb, :], in_=x_d[r0:r0 + 128, :]),
                    writes=[b_xres[tb]], dma=b_xres[tb])
            mode = 'main'
            if NPRE > 0:
                mode = 'pre' if t < NPRE else ('m0' if t == NPRE else 'main')
            first_real = (t == (NPRE + 1 if NPRE > 0 else 0))
            P.handoff(L1_BUFS, L0_BUFS)
            layer0(mode)
            if mode != 'pre':
                P.handoff(L0_BUFS, L1_BUFS)
                layer1(mode, first_real)
            if mode == 'm0':
                fl = flag[:, 0:1]
                for g in range(NG):
                    add("dve", ts_fn(S[:, g, :], S[:, g, :], fl, None, ALU.mult), reads=[b_S[g], b_flag], writes=[b_S[g]])
                add("dve", ts_fn(chalo[:].rearrange("p a b -> p (a b)"), chalo[:].rearrange("p a b -> p (a b)"), fl, None, ALU.mult),
                    reads=b_chalo + [b_flag], writes=b_chalo)
                add("dve", ts_fn(uhalo[:].rearrange("p a b -> p (a b)"), uhalo[:].rearrange("p a b -> p (a b)"), fl, None, ALU.mult),
                    reads=b_uhalo + [b_flag], writes=b_uhalo)
            if mode == 'main':
                final_store(t)
        add("sp", None, writes=b_xres)
        P.emit()
        print("ops:", {e: len(P.ops[e]) for e in ENGS}, "scratch", l0_end, off["v"], flush=True)
    return nc


def _consts(first_half=True):
    t = np.arange(128)
    ident = (t[:, None] == t[None, :]).astype(np.float32)
    ones = np.ones((128, 128), np.float32)
    ntri = -(t[:, None] <= t[None, :]).astype(np.float32)
    trile = (t[:, None] <= t[None, :]).astype(np.float32)
    trigt = (t[:, None] > t[None, :]).astype(np.float32)
    nmask = np.where(t[:, None] > t[None, :], NEG, 0.0).astype(np.float32)
    cmask = np.stack([ident, ones, ntri, trile, trigt, nmask], axis=1)
    invc = np.zeros((1, 4, 16), np.float32)
    for gi, w in enumerate((2, 4, 8, 16)):
        invc[0, gi] = 1.0 / (np.minimum(np.arange(1, 17), w) if first_half else np.full(16, w))
    return np.ascontiguousarray(cmask), invc.reshape(1, 64)


def _fm(v, nblk):
    return np.ascontiguousarray(np.asarray(v, np.float32).reshape(nblk, 128).T)


_NC_CACHE = {}
HALF = SEQ // 2


def core_inputs(x, common, b, h):
    m = dict(common)
    if h == 0:
        m["x"] = np.concatenate([np.zeros((HALF, D), np.float32), x[b, :HALF]], axis=0)
    else:
        m["x"] = np.ascontiguousarray(x[b])
    cm, invc = _consts(first_half=(h == 0))
    m["cmask"] = cm
    m["invc"] = invc
    m["flag"] = np.full((128, 1), float(h), np.float32)
    return m


def common_inputs(ln_g, final_g, ssm_w_in, ssm_conv_w, ssm_conv_b, ssm_dt_bias, ssm_a_log, ssm_d, ssm_norm_g,
                  ssm_w_out, pool_w_in, pool_w_group, pool_scale, pool_w_out):
    f = lambda a: np.ascontiguousarray(np.asarray(a, dtype=np.float32))
    convw = np.ascontiguousarray(f(ssm_conv_w)[0].T.reshape(48, 128, 4).transpose(1, 0, 2))
    convb = _fm(f(ssm_conv_b)[0], 48)
    rows = np.concatenate([f(ssm_dt_bias)[0], f(ssm_a_log)[0], f(ssm_d)[0]])[None, :]
    lng = np.ascontiguousarray(f(ln_g).reshape(2, 16, 128).transpose(2, 0, 1))
    return {
        "ssm_w_in": f(ssm_w_in)[0], "ssm_w_out": f(ssm_w_out)[0], "pool_w_in": f(pool_w_in)[0],
        "pool_w_group": np.ascontiguousarray(f(pool_w_group)[0].reshape(4096, 1024)), "pool_w_out": f(pool_w_out)[0],
        "convw": convw, "convb": convb, "rows": np.ascontiguousarray(rows),
        "lng": lng, "fg": f(final_g)[None, :], "ng": _fm(f(ssm_norm_g)[0], 32), "psc": _fm(f(pool_scale)[0], 32),
    }


def kernel(x, ln_g, final_g, ssm_w_in, ssm_conv_w, ssm_conv_b, ssm_dt_bias, ssm_a_log, ssm_d, ssm_norm_g,
           ssm_w_out, pool_w_in, pool_w_group, pool_scale, pool_w_out):
    x = np.ascontiguousarray(np.asarray(x, dtype=np.float32))
    if "nc" not in _NC_CACHE:
        _NC_CACHE["nc"] = build_program(3, 5)
    nc = _NC_CACHE["nc"]
    common = common_inputs(ln_g, final_g, ssm_w_in, ssm_conv_w, ssm_conv_b, ssm_dt_bias, ssm_a_log, ssm_d, ssm_norm_g,
                           ssm_w_out, pool_w_in, pool_w_group, pool_scale, pool_w_out)
    in_maps = [core_inputs(x, common, c // 2, c % 2) for c in range(8)]
    res = run_bass_kernel_spmd(nc, in_maps, core_ids=list(range(8)))
    out = np.empty((BATCH, SEQ, D), np.float32)
    for c in range(8):
        out[c // 2, (c % 2) * HALF:(c % 2 + 1) * HALF] = np.asarray(res.results[c]["out"], dtype=np.float32)
    return out
```

```python
import numpy as np
from contextlib import ExitStack
import concourse.bass as bass
import concourse.mybir as mybir
from concourse.bass_utils import run_bass_kernel_spmd

F32 = mybir.dt.float32
BF16 = mybir.dt.bfloat16
AF = mybir.ActivationFunctionType
ALU = mybir.AluOpType
ENGS = ("pe", "act", "dve", "pool", "sp")

D = 2048
SEQ = 4096
BATCH = 4
DIN = 4096
NH = 64
NG = 8
NST = 128
CONV_DIM = 6144
SSM_IN = 10304
TT = 512
EPS = 1e-6
NEG = -30000.0


class Buf:
    __slots__ = ("name", "w", "readers", "dsem", "dcnt", "grp", "excl")

    def __init__(self, name):
        self.name = name
        self.w = None
        self.readers = {}
        self.dsem = None
        self.dcnt = None
        self.grp = None
        self.excl = False


class Op:
    __slots__ = ("eng", "fn", "deps", "sig", "is_dma", "need", "waits")

    def __init__(self, eng, fn, is_dma):
        self.eng = eng
        self.fn = fn
        self.deps = []
        self.sig = None
        self.is_dma = is_dma
        self.need = False
        self.waits = []


class Prog:
    def __init__(self, nc, es):
        self.nc = nc
        self.es = es
        self.ops = {e: [] for e in ENGS}
        self.sem = {e: es.enter_context(nc.semaphore("c_" + e)) for e in ("pe", "act", "dve", "pool")}
        self.groups = []
        self.nd = 0

    def new_dsem(self):
        self.nd += 1
        return [self.es.enter_context(self.nc.semaphore("d%d" % self.nd))], [0]

    def share_dsem(self, bufs, group=False):
        s, c = self.new_dsem()
        g = [s, c, []] if group else None
        if group:
            self.groups.append(g)
        for b in bufs:
            b.dsem, b.dcnt = s, c
            b.grp = g

    def handoff(self, olds, news):
        ops = {}
        for b in olds:
            if b.w is not None:
                ops[id(b.w)] = b.w
            for r in b.readers.values():
                ops[id(r)] = r
        for nb in news:
            for k, o in ops.items():
                nb.readers[k] = o

    def add(self, eng, fn, reads=(), writes=(), dma=None):
        op = Op(eng, fn, dma is not None)
        deps = op.deps
        for b in reads:
            if b.w is not None:
                deps.append((b.w, 0))
            if b.excl:
                for k, r in b.readers.items():
                    if k != eng:
                        deps.append((r, 2))
        for b in writes:
            if b.w is not None:
                deps.append((b.w, 1))
            for r in b.readers.values():
                if r is not op:
                    deps.append((r, 2))
        for b in writes:
            b.w = op
            b.readers = {}
        for b in reads:
            if b.w is op:
                continue
            key = eng if dma is None else id(op)
            b.readers[key] = op
        if dma is not None:
            if dma.dsem is None:
                dma.dsem, dma.dcnt = self.new_dsem()
            dma.dcnt[0] += 16
            op.sig = [dma.dsem[0], dma.dcnt[0]]
            if dma.grp is not None:
                dma.grp[2].append(op)
        self.ops[eng].append(op)
        return op

    def emit(self):
        nc = self.nc
        for g in self.groups:
            for op in g[2]:
                op.sig[1] = g[1][0]
        for e in ENGS:
            for op in self.ops[e]:
                for d, kind in op.deps:
                    if d.eng == op.eng and not d.is_dma and not op.is_dma:
                        if e == "pe" or kind != 0:
                            continue
                    d.need = True
                    op.waits.append(d)
        for e in ENGS:
            cnt = 0
            for op in self.ops[e]:
                if op.is_dma:
                    continue
                if op.need:
                    cnt += 1
                    op.sig = [self.sem[e], cnt]

        def run(e, eng):
            waited = {}
            for op in self.ops[e]:
                for d in op.waits:
                    s, v = d.sig
                    k = id(s)
                    if waited.get(k, 0) < v:
                        eng.wait_ge(s, v)
                        waited[k] = v
                if op.fn is None:
                    continue
                ins = op.fn(eng)
                if op.is_dma:
                    ins.then_inc(op.sig[0], 16)
                elif op.need:
                    ins.then_inc(op.sig[0], 1)

        with nc.Block() as block:
            @block.tensor
            def _(eng):
                run("pe", eng)

            @block.scalar
            def _(eng):
                run("act", eng)

            @block.vector
            def _(eng):
                run("dve", eng)

            @block.gpsimd
            def _(eng):
                run("pool", eng)

            @block.sync
            def _(eng):
                run("sp", eng)


def build_program(NPRE=3, NMAIN=5, layers=(0, 1), final=True):
    NT = NPRE + NMAIN
    ntok = NT * TT
    nout = (NMAIN - 1) * TT if NPRE > 0 else NMAIN * TT
    nc = bass.Bass("TRN2", target_bir_lowering=False)
    dr = lambda n, s: nc.dram_tensor(n, s, F32, kind="ExternalInput").ap()
    x_d = dr("x", [ntok, D])
    w_in0 = dr("ssm_w_in", [D, SSM_IN])
    w_out0 = dr("ssm_w_out", [DIN, D])
    w_in1 = dr("pool_w_in", [D, 2 * DIN])
    w_grp = dr("pool_w_group", [DIN, 1024])
    w_out1 = dr("pool_w_out", [DIN, D])
    cmask_d = dr("cmask", [128, 6, 128])
    convw_d = dr("convw", [128, 48, 4])
    convb_d = dr("convb", [128, 48])
    rows_d = dr("rows", [1, 3 * 64])
    lng_d = dr("lng", [128, 2, 16])
    fg_d = dr("fg", [1, D])
    ng_d = dr("ng", [128, 32])
    psc_d = dr("psc", [128, 32])
    invc_d = dr("invc", [1, 64])
    flag_d = dr("flag", [128, 1])
    out_d = nc.dram_tensor("out", [nout, D], F32, kind="ExternalOutput").ap()

    with ExitStack() as es:
        P = Prog(nc, es)
        add = P.add

        def SB(name, shape, dt):
            return es.enter_context(nc.sbuf_tensor("s_" + name, shape, dt))

        def PS(name, shape, dt):
            return es.enter_context(nc.psum_tensor("p_" + name, shape, dt))

        rr = {"i": 0}

        def ev():
            rr["i"] += 1
            return "act" if rr["i"] & 1 else "dve"

        cmask = SB("cmask", [128, 6, 128], F32)
        cmb = SB("cmb", [128, 6, 128], BF16)
        convw = SB("convw", [128, 48, 4], F32)
        convb = SB("convb", [128, 48], F32)
        rows = SB("rows", [128, 3, 64], F32)
        arow = SB("arow", [128, 64], F32)
        lng = SB("lng", [128, 2, 16], F32)
        fgrow = SB("fgrow", [128, D], F32)
        ngf = SB("ngf", [128, 32], F32)
        pscf = SB("pscf", [128, 32], F32)
        invc = SB("invc", [128, 4, 16], F32)
        wdt = SB("wdt", [128, 16, 64], BF16)
        flag = SB("flag", [128, 1], F32)
        b_const = [Buf("c%d" % i) for i in range(11)]
        P.share_dsem(b_const, group=True)
        (b_cmask, b_convw, b_convb, b_rows, b_lng, b_fg, b_ng, b_psc, b_invc, b_wdt, b_flag) = b_const
        add("sp", lambda e: e.dma_start(out=flag[:], in_=flag_d), writes=[b_flag], dma=b_flag)
        add("sp", lambda e: e.dma_start(out=cmask[:], in_=cmask_d), writes=[b_cmask], dma=b_cmask)
        add("sp", lambda e: e.dma_start(out=convw[:], in_=convw_d), writes=[b_convw], dma=b_convw)
        add("sp", lambda e: e.dma_start(out=convb[:], in_=convb_d), writes=[b_convb], dma=b_convb)
        add("sp", lambda e: e.dma_start(out=rows[:].rearrange("p a b -> p (a b)"), in_=rows_d.partition_broadcast(128)),
            writes=[b_rows], dma=b_rows)
        add("sp", lambda e: e.dma_start(out=lng[:], in_=lng_d), writes=[b_lng], dma=b_lng)
        add("sp", lambda e: e.dma_start(out=fgrow[:], in_=fg_d.partition_broadcast(128)), writes=[b_fg], dma=b_fg)
        add("sp", lambda e: e.dma_start(out=ngf[:], in_=ng_d), writes=[b_ng], dma=b_ng)
        add("sp", lambda e: e.dma_start(out=pscf[:], in_=psc_d), writes=[b_psc], dma=b_psc)
        add("sp", lambda e: e.dma_start(out=invc[:].rearrange("p a b -> p (a b)"), in_=invc_d.partition_broadcast(128)),
            writes=[b_invc], dma=b_invc)
        add("pool", lambda e: e.dma_start(out=wdt[:], in_=w_in0[:, 10240:10304].rearrange("(kc p) n -> p kc n", p=128)),
            writes=[b_wdt], dma=b_wdt)
        b_cmb = Buf("cmb")
        add("dve", lambda e: e.tensor_copy(out=cmb[:], in_=cmask[:]), reads=[b_cmask], writes=[b_cmb])
        b_arow = Buf("arow")
        add("act", lambda e: e.activation(out=arow[:], in_=rows[:, 1, :], func=AF.Exp), reads=[b_rows], writes=[b_arow])
        add("dve", lambda e: e.tensor_scalar(out=arow[:], in0=arow[:], scalar1=-1.0, scalar2=None, op0=ALU.mult),
            reads=[b_arow], writes=[b_arow])
        IDENT, ONES, NTRI, TRILE, TRIGT, NMASK = range(6)

        xres = SB("xres", [128, 4, D], F32)
        b_xres = [Buf("xres%d" % i) for i in range(4)]
        hnT = SB("hnT", [128, 16, TT], BF16)
        b_hnT = [Buf("hnT%d" % i) for i in range(4)]
        hnst = SB("hnst", [128, D], BF16)
        b_hnst = Buf("hnst")
        small = SB("small", [128, 64], F32)
        b_small = [Buf("small%d" % i) for i in range(64)]
        S = SB("S", [128, NG, 512], F32)
        b_S = [Buf("S%d" % g) for g in range(NG)]
        Sbf = SB("Sbf", [128, 2, 512], BF16)
        b_Sbf = [Buf("Sbf%d" % g) for g in range(2)]
        ygT = SB("ygT", [128, 32, TT], BF16)
        b_ygT = [Buf("ygT%d" % k) for k in range(32)]
        ring = SB("ring", [128, 4, 4096], BF16)
        b_ring = [Buf("ring%d" % i) for i in range(4)]
        chalo = SB("chalo", [128, 48, 3], F32)
        b_chalo = [Buf("chalo%d" % i) for i in range(48)]
        uhalo = SB("uhalo", [128, 32, 15], F32)
        b_uhalo = [Buf("uhalo%d" % i) for i in range(32)]
        ssq = SB("ssq", [128, 4, 8], F32)
        b_ssq = [Buf("ssq%d" % i) for i in range(4)]
        dtf = SB("dtf", [128, 4, 5, 64], F32)
        adtb = SB("adtb", [128, 4, 64], BF16)
        b_dt = [[Buf("dt%d_%d" % (t, j)) for j in range(6)] for t in range(4)]
        dtmp = SB("dtmp", [128, 3, 64], F32)
        b_dtmp = [Buf("dtmp%d" % j) for j in range(3)]

        SCR_BYTES = 38 * 1024
        scr = SB("scr", [128, SCR_BYTES // 2], BF16)
        off = {"v": 0}

        def carve(nelem, dt, shape=None):
            nb = nelem * (4 if dt == F32 else 2)
            o = off["v"]
            off["v"] = o + ((nb + 31) // 32) * 32
            assert off["v"] <= SCR_BYTES, off["v"]
            ap = scr[:, o // 2:(o + nb) // 2]
            if dt == F32:
                ap = ap.bitcast(F32)
            return ap

        raw = [carve(515, F32) for _ in range(2)]
        acc = [carve(512, F32) for _ in range(2)]
        BT2 = [carve(512, BF16) for _ in range(2)]
        CT2 = [carve(512, BF16) for _ in range(2)]
        Btok = carve(512, BF16)
        sz2 = [carve(2048, BF16) for _ in range(2)]
        xdt = [carve(512, BF16)] * 2
        xD = [carve(512, BF16)] * 2
        xw = [carve(512, BF16)] * 2
        rhs1 = [carve(1024, BF16)] * 2
        decT = [carve(1024, BF16)] * 2
        MT = [carve(1024, BF16)] * 2
        yoffb = carve(512, BF16)
        yg = [carve(512, BF16) for _ in range(2)]
        sqj = carve(512, BF16)
        l0_end = off["v"]
        b_raw = [Buf("raw%d" % i) for i in range(2)]
        b_acc = [Buf("acc%d" % i) for i in range(2)]
        b_BT2, b_CT2 = [Buf("BT0"), Buf("BT1")], [Buf("CT0"), Buf("CT1")]
        b_Btok = [Buf("Btok%d" % i) for i in range(4)]
        b_sz2 = [[Buf("sz%d_%d" % (p_, i)) for i in range(4)] for p_ in range(2)]
        b_xdt = [Buf("xdt")] * 2
        b_xD = [Buf("xD")] * 2
        b_xw = [Buf("xw")] * 2
        b_rhs1 = [Buf("rhs1")] * 2
        b_decT = [Buf("decT")] * 2
        b_MT = [Buf("MT")] * 2
        b_yoffb = Buf("yoffb")
        b_yg = [Buf("yg%d" % i) for i in range(2)]
        b_sqj = Buf("sqj")
        L0_BUFS = (b_raw + b_acc + b_BT2 + b_CT2 + b_Btok + b_sz2[0] + b_sz2[1] + b_xdt[:1] + b_xD[:1] + b_xw[:1] + b_rhs1[:1] +
                   b_decT[:1] + b_MT[:1] + [b_yoffb] + b_yg + [b_sqj])
        off["v"] = 0
        uraw = [carve(527, F32) for _ in range(2)]
        s2 = carve(527, F32)
        s4 = carve(527, F32)
        s8 = carve(527, F32)
        mixT = carve(8 * 512, BF16)
        sgT = carve(8 * 512, BF16)
        utmp = carve(16, F32)
        b_uraw = [Buf("uraw%d" % i) for i in range(2)]
        b_s2, b_s4, b_s8, b_utmp = Buf("s2"), Buf("s4"), Buf("s8"), Buf("utmp")
        b_mixT = [Buf("mixT%d" % i) for i in range(8)]
        b_sgT = [Buf("sgT%d" % i) for i in range(8)]
        L1_BUFS = b_uraw + [b_s2, b_s4, b_s8, b_utmp] + b_mixT + b_sgT

        pA = [PS("pA%d" % i, [128, 512], F32) for i in range(2)]
        b_pA = [Buf("pA%d" % i) for i in range(2)]
        pT = PS("pT", [128, 1024], BF16)
        b_pT = [Buf("pT")] * 2
        pC = PS("pC", [128, 512], F32)
        b_pC = [Buf("pC")] * 4
        pSeg = PS("pSeg", [128, 1024], F32)
        b_pSeg = Buf("pSeg")
        pY = PS("pY", [128, 512], F32)
        b_pY = Buf("pY")
        pO = PS("pO", [128, 512], F32)
        b_pO = Buf("pO")
        acc_banks = [(pA[0], b_pA[0]), (pA[1], b_pA[1]), (pY, b_pY), (pO, b_pO)]
        for _b in b_pA + b_pT + b_pC + [b_pSeg, b_pY, b_pO]:
            _b.excl = True
        tr_banks = [(pT[:, 0:512], b_pT[0]), (pY[:, 0:256].bitcast(BF16), b_pY),
                    (pO[:, 0:256].bitcast(BF16), b_pO), (pSeg[:, 0:256].bitcast(BF16), b_pSeg)]

        b_init = Buf("init")
        add("dve", lambda e: e.memset(S[:], 0.0), writes=b_S)
        add("dve", lambda e: e.memset(Sbf[:], 0.0), writes=b_Sbf)
        add("dve", lambda e: e.memset(chalo[:], 0.0), writes=b_chalo)
        add("dve", lambda e: e.memset(uhalo[:], 0.0), writes=b_uhalo)

        ringst = {"i": 0}

        def load_slab(src_ap, nkc, ncols):
            i = ringst["i"] % 4
            ringst["i"] += 1
            view = ring[:, i, 0:nkc * ncols].rearrange("p (k n) -> p k n", k=nkc)
            src = src_ap.rearrange("(k p) n -> p k n", p=128)
            add("pool", lambda e: e.dma_start(out=view, in_=src), writes=[b_ring[i]], dma=b_ring[i])
            return view, b_ring[i]

        def mm_fn(out_ap, pairs, first=True, last=True):
            def fn(e):
                n = len(pairs)
                ins = None
                for j, (l, r) in enumerate(pairs):
                    ins = e.matmul(out_ap, lhsT=l, rhs=r, start=(first and j == 0), stop=(last and j == n - 1))
                return ins
            return fn

        def act_fn(out, in_, func, **kw):
            return lambda e: e.activation(out=out, in_=in_, func=func, **kw)

        def copy_on(eng, out, in_, reads, writes):
            if eng == "act":
                add("act", lambda e: e.activation(out=out, in_=in_, func=AF.Copy), reads=reads, writes=writes)
            else:
                add(eng, lambda e: e.tensor_copy(out=out, in_=in_), reads=reads, writes=writes)

        def tt_fn(out, in0, in1, op):
            return lambda e: e.tensor_tensor(out=out, in0=in0, in1=in1, op=op)

        def ts_fn(out, in0, s1, s2, op0, op1=None, **kw):
            if op1 is None:
                return lambda e: e.tensor_scalar(out=out, in0=in0, scalar1=s1, scalar2=None, op0=op0, **kw)
            return lambda e: e.tensor_scalar(out=out, in0=in0, scalar1=s1, scalar2=s2, op0=op0, op1=op1, **kw)

        def stt_fn(out, in0, scalar, in1, op0, op1, **kw):
            return lambda e: e.scalar_tensor_tensor(out=out, in0=in0, scalar=scalar, in1=in1, op0=op0, op1=op1, **kw)

        def norm_to_hnT(layer):
            for tb in range(4):
                ss = small[:, tb * 4:tb * 4 + 1]
                rs = small[:, tb * 4 + 1:tb * 4 + 2]
                b_ss, b_rs = b_small[tb * 4], b_small[tb * 4 + 1]
                add("dve", stt_fn(hnst[:], xres[:, tb, :], 1.0, xres[:, tb, :], ALU.mult, ALU.mult, accum_out=ss),
                    reads=[b_xres[tb]], writes=[b_hnst, b_ss])
                add("act", act_fn(rs, ss, AF.Sqrt, scale=1.0 / D, bias=epsb[:, 0:1]), reads=[b_ss, b_eps], writes=[b_rs])
                add("dve", lambda e, rs=rs: e.reciprocal(out=rs, in_=rs), reads=[b_rs], writes=[b_rs])
                add("act", act_fn(hnst[:], xres[:, tb, :], AF.Copy, scale=rs), reads=[b_xres[tb], b_rs], writes=[b_hnst])
                for rnd in range(4):
                    pt, b_pt = tr_banks[rnd]

                    def trf(e, rnd=rnd, pt=pt):
                        ins = None
                        for j in range(4):
                            kc = rnd * 4 + j
                            ins = e.transpose(out=pt[:, j * 128:(j + 1) * 128], in_=hnst[:, kc * 128:(kc + 1) * 128],
                                              identity=cmb[:, IDENT, :])
                        return ins
                    add("pe", trf, reads=[b_hnst, b_cmb], writes=[b_pt])
                    o = hnT[:, rnd * 4:rnd * 4 + 4, tb * 128:(tb + 1) * 128]
                    g_bc = lng[:, layer, rnd * 4:rnd * 4 + 4].unsqueeze(2).to_broadcast([128, 4, 128])
                    add("dve", tt_fn(o, pt.rearrange("p (a b) -> p a b", a=4), g_bc, ALU.mult),
                        reads=[b_pt, b_lng], writes=[b_hnT[tb]])

        epsb = SB("epsb", [128, 1], F32)
        b_eps = Buf("eps")
        oneb = SB("oneb", [128, 1], F32)

        def _mk_eps(e):
            e.memset(oneb[:], 1.0)
            return e.memset(epsb[:], EPS)
        add("dve", _mk_eps, writes=[b_eps])

        def out_proj(w_ap, rstd_aps, rstd_bufs):
            for cb in range(4):
                for s in range(4):
                    view, rb = load_slab(w_ap[s * 1024:(s + 1) * 1024, cb * 512:(cb + 1) * 512], 8, 512)
                    for tb in range(4):
                        pt_, bpt = acc_banks[tb]
                        pairs = [(ygT[:, s * 8 + k, tb * 128:(tb + 1) * 128], view[:, k, :]) for k in range(8)]
                        add("pe", mm_fn(pt_[:], pairs, first=(s == 0), last=(s == 3)),
                            reads=[rb] + b_ygT[s * 8:s * 8 + 8], writes=[bpt])
                for tb in range(4):
                    pt_, bpt = acc_banks[tb]
                    xr = xres[:, tb, cb * 512:(cb + 1) * 512]
                    if rstd_aps is None:
                        add("dve", tt_fn(xr, pt_[:], xr, ALU.add), reads=[bpt, b_xres[tb]], writes=[b_xres[tb]])
                    else:
                        add("dve", stt_fn(xr, pt_[:], rstd_aps[tb], xr, ALU.mult, ALU.add),
                            reads=[bpt, b_xres[tb], rstd_bufs[tb]], writes=[b_xres[tb]])

        def layer0(mode):
            pre = (mode == 'pre')
            norm_to_hnT(0)
            for tb in range(4):
                DT, ADT, ETOK, WEXP, EL = range(5)
                bd = b_dt[tb]
                pairs = [(hnT[:, kc, tb * 128:(tb + 1) * 128], wdt[:, kc, :]) for kc in range(16)]
                add("pe", mm_fn(pA[0][:, 0:64], pairs), reads=[b_hnT[tb], b_wdt], writes=[b_pA[0]])
                v = dtmp[:, 0, :]
                a = dtmp[:, 1, :]
                l = dtmp[:, 2, :]
                add("dve", tt_fn(v, pA[0][:, 0:64], rows[:, 0, :], ALU.add), reads=[b_pA[0], b_rows], writes=[b_dtmp[0]])
                add("dve", stt_fn(a, v, -1.0, v, ALU.mult, ALU.max), reads=[b_dtmp[0]], writes=[b_dtmp[1]])
                add("act", act_fn(a, a, AF.Exp, scale=-1.0), reads=[b_dtmp[1]], writes=[b_dtmp[1]])
                add("act", act_fn(l, a, AF.Ln, bias=oneb[:, 0:1]), reads=[b_dtmp[1], b_eps], writes=[b_dtmp[2]])
                add("dve", stt_fn(dtf[:, tb, DT, :], v, 0.0, l, ALU.max, ALU.add), reads=[b_dtmp[0], b_dtmp[2]], writes=[bd[DT]])
                add("dve", tt_fn(dtf[:, tb, ADT, :], dtf[:, tb, DT, :], arow[:], ALU.mult), reads=[bd[DT], b_arow], writes=[bd[ADT]])
                copy_on("dve", adtb[:, tb, :], dtf[:, tb, ADT, :], [bd[ADT]], [bd[5]])
                for j, (msk, dst) in enumerate(((TRILE, ETOK), (TRIGT, WEXP), (ONES, EL))):
                    pc = pC[:, 128 + j * 64:128 + (j + 1) * 64]
                    add("pe", mm_fn(pc, [(cmask[:, msk, :], dtf[:, tb, ADT, :])]), reads=[b_cmask, bd[ADT]], writes=[b_pC[1 + j]])
                    add("act", act_fn(dtf[:, tb, dst, :], pc, AF.Exp), reads=[b_pC[1 + j]], writes=[bd[dst]])
            for tb in range(4):
                add("dve", lambda e, tb=tb: e.memset(ssq[:, tb, :], 0.0), writes=[b_ssq[tb]])

            blk = {"i": 0}

            def conv_block(ps_ap, b_ps, cbi, out_ap, b_out):
                i = blk["i"] & 1
                blk["i"] += 1
                r, a_ = raw[i], acc[i]
                add("dve", lambda e: e.tensor_copy(out=r[:, 0:3], in_=chalo[:, cbi, :]), reads=[b_chalo[cbi]], writes=[b_raw[i]])
                add("act", act_fn(r[:, 3:515], ps_ap, AF.Copy), reads=[b_ps], writes=[b_raw[i]])
                add("dve", lambda e: e.tensor_copy(out=chalo[:, cbi, :], in_=r[:, 512:515]), reads=[b_raw[i]], writes=[b_chalo[cbi]])
                add("dve", ts_fn(a_, r[:, 0:512], convw[:, cbi, 0:1], None, ALU.mult), reads=[b_raw[i], b_convw], writes=[b_acc[i]])
                for k in range(1, 4):
                    add("dve", stt_fn(a_, r[:, k:k + 512], convw[:, cbi, k:k + 1], a_, ALU.mult, ALU.add),
                        reads=[b_raw[i], b_convw, b_acc[i]], writes=[b_acc[i]])
                add("act", act_fn(out_ap, a_, AF.Silu, bias=convb[:, cbi:cbi + 1]), reads=[b_acc[i], b_convb], writes=[b_out])

            pa = {"i": 0}

            def next_pa():
                i = pa["i"] & 3
                pa["i"] += 1
                return acc_banks[i]

            def A_thunks(g):
                par = g & 1
                th = []
                if not pre:
                    stz = {}

                    def z_thunk(tb, stz=stz):
                        if tb == 0:
                            stz["a"] = load_slab(w_in0[0:1024, g * 512:(g + 1) * 512], 8, 512)
                            stz["b"] = load_slab(w_in0[1024:2048, g * 512:(g + 1) * 512], 8, 512)
                        (v0, r0), (v1, r1) = stz["a"], stz["b"]
                        pp, bpp = next_pa()
                        pairs = [(hnT[:, kc, tb * 128:(tb + 1) * 128], (v0 if kc < 8 else v1)[:, kc % 8, :]) for kc in range(16)]
                        add("pe", mm_fn(pp[:], pairs), reads=[b_hnT[tb], r0, r1], writes=[bpp])
                        add("act", act_fn(sz2[par][:, tb * 512:(tb + 1) * 512], pp[:], AF.Silu), reads=[bpp], writes=[b_sz2[par][tb]])
                    for tb in range(4):
                        th.append(lambda tb=tb: z_thunk(tb))
                stx = {}

                def x_thunk(c, stx=stx):
                    c0 = 4096 + g * 512
                    if c == 0:
                        stx["a"] = load_slab(w_in0[0:1024, c0:c0 + 512], 8, 512)
                        stx["b"] = load_slab(w_in0[1024:2048, c0:c0 + 512], 8, 512)
                    (v0, r0), (v1, r1) = stx["a"], stx["b"]
                    pp, bpp = next_pa()
                    pairs = [((v0 if kc < 8 else v1)[:, kc % 8, c * 128:(c + 1) * 128], hnT[:, kc, :]) for kc in range(16)]
                    add("pe", mm_fn(pp[:], pairs), reads=b_hnT + [r0, r1], writes=[bpp])
                    conv_block(pp[:], bpp, g * 4 + c, xact42[par][c], b_xact42[par][c])
                for c in range(4):
                    th.append(lambda c=c: x_thunk(c))

                def bc_thunk(which):
                    col = (8192 if which == 0 else 9216) + g * 128
                    vB, rB = load_slab(w_in0[:, col:col + 128], 16, 128)
                    pp, bpp = next_pa()
                    add("pe", mm_fn(pp[:], [(vB[:, kc, :], hnT[:, kc, :]) for kc in range(16)]), reads=b_hnT + [rB], writes=[bpp])
                    if which == 0:
                        conv_block(pp[:], bpp, 32 + g, BT2[par], b_BT2[par])
                    else:
                        conv_block(pp[:], bpp, 40 + g, CT2[par], b_CT2[par])
                th.append(lambda: bc_thunk(0))
                th.append(lambda: bc_thunk(1))
                return th

            pending = []

            def pump():
                if pending:
                    pending.pop(0)()

            for th_ in A_thunks(0):
                th_()
            for g in range(NG):
                par = g & 1
                xact4, b_xact4 = xact42[par], b_xact42[par]
                BT, CT, b_BT, b_CT = BT2[par], CT2[par], b_BT2[par], b_CT2[par]
                sz, b_sz = sz2[par], b_sz2[par]
                pending.extend(A_thunks(g + 1) if g + 1 < NG else [])
                while pending:
                    pump()
                if not pre:
                    copy_on("act", Sbf[:, g & 1, :], S[:, g, :], [b_S[g]], [b_Sbf[g & 1]])
                for tb in range(4):
                    DT, ADT, ETOK, WEXP, EL = range(5)
                    bd = b_dt[tb]
                    i2 = tb & 1
                    tsl = slice(tb * 128, (tb + 1) * 128)
                    hs = slice(g * 8, (g + 1) * 8)
                    ph = pT[:, 0:512]
                    pb = pT[:, 512:640]

                    def trx(e, tsl=tsl, ph=ph, pb=pb, xact4=xact4, BT=BT):
                        for c in range(4):
                            e.transpose(out=ph[:, c * 128:(c + 1) * 128], in_=xact4[c][:, tsl], identity=cmb[:, IDENT, :])
                        return e.transpose(out=pb, in_=BT[:, tsl], identity=cmb[:, IDENT, :])
                    add("pe", trx, reads=b_xact4 + [b_BT, b_cmb], writes=[b_pT[0]])
                    pump()
                    ph3 = ph.rearrange("p (h q) -> p h q", h=8)
                    dt_bc = dtf[:, tb, DT, hs].unsqueeze(2).to_broadcast([128, 8, 64])
                    d_bc = rows[:, 2, hs].unsqueeze(2).to_broadcast([128, 8, 64])
                    add("dve", tt_fn(xdt[i2].rearrange("p (h q) -> p h q", h=8), ph3, dt_bc, ALU.mult),
                        reads=[b_pT[0], bd[DT]], writes=[b_xdt[i2]])
                    if not pre:
                        add("dve", tt_fn(xD[i2].rearrange("p (h q) -> p h q", h=8), ph3, d_bc, ALU.mult),
                            reads=[b_pT[0], b_rows], writes=[b_xD[i2]])
                    add("dve", lambda e, tb=tb, pb=pb: e.tensor_copy(out=Btok[:, tb * 128:(tb + 1) * 128], in_=pb),
                        reads=[b_pT[0]], writes=[b_Btok[tb]])
                    if not pre:
                        add("pe", mm_fn(pC[:, 0:128], [(BT[:, tsl], CT[:, tsl])]), reads=[b_BT, b_CT], writes=[b_pC[0]])
                        r1v = rhs1[i2].rearrange("p (h l) -> p h l", h=8)
                        add("dve", tt_fn(r1v, cmb[:, TRILE, :].unsqueeze(1).to_broadcast([128, 8, 128]),
                                         dtf[:, tb, ADT, hs].unsqueeze(2).to_broadcast([128, 8, 128]), ALU.mult),
                            reads=[b_cmb, bd[ADT]], writes=[b_rhs1[i2]])

                        def segf(e, tb=tb, r1v=r1v, g=g):
                            ins = None
                            for hh in range(2):
                                o = pSeg[:, hh * 512:(hh + 1) * 512].rearrange("p (h l) -> p h l", h=4)
                                e.matmul(o, lhsT=cmb[:, ONES, :], rhs=r1v[:, hh * 4:(hh + 1) * 4, :], start=True, stop=False)
                                e.matmul(o, lhsT=cmb[:, NTRI, :],
                                         rhs=adtb[:, tb, g * 8 + hh * 4:g * 8 + hh * 4 + 4].unsqueeze(2).to_broadcast([128, 4, 128]),
                                         start=False, stop=False)
                                ins = e.matmul(o, lhsT=cmb[:, IDENT, :],
                                               rhs=cmb[:, NMASK, :].unsqueeze(1).to_broadcast([128, 4, 128]), start=False, stop=True)
                            return ins
                        add("pe", segf, reads=[b_cmb, b_rhs1[i2], bd[5]], writes=[b_pSeg])
                        pump()
                        add("act", act_fn(decT[i2], pSeg[:], AF.Exp), reads=[b_pSeg], writes=[b_decT[i2]])
                        mtv = MT[i2].rearrange("p (h l) -> p h l", h=8)
                        add("dve", tt_fn(mtv, decT[i2].rearrange("p (h l) -> p h l", h=8),
                                         pC[:, 0:128].unsqueeze(1).to_broadcast([128, 8, 128]), ALU.mult),
                            reads=[b_decT[i2], b_pC[0]], writes=[b_MT[i2]])
                        add("pe", mm_fn(pO[:], [(CT[:, tsl], Sbf[:, g & 1, :])]), reads=[b_CT, b_Sbf[g & 1]], writes=[b_pO])
                        add("dve", tt_fn(yoffb.rearrange("p (h q) -> p h q", h=8), pO[:].rearrange("p (h q) -> p h q", h=8),
                                         dtf[:, tb, ETOK, hs].unsqueeze(2).to_broadcast([128, 8, 64]), ALU.mult),
                            reads=[b_pO, bd[ETOK]], writes=[b_yoffb])

                        def yf(e, i2=i2, mtv=mtv):
                            e.matmul(pY[:], lhsT=cmb[:, IDENT, :], rhs=xD[i2], start=True, stop=False)
                            for h in range(8):
                                e.matmul(pY[:, h * 64:(h + 1) * 64], lhsT=mtv[:, h, :], rhs=xdt[i2][:, h * 64:(h + 1) * 64],
                                         start=False, stop=False)
                            return e.matmul(pY[:], lhsT=cmb[:, IDENT, :], rhs=yoffb, start=False, stop=True)
                        add("pe", yf, reads=[b_cmb, b_xD[i2], b_xdt[i2], b_MT[i2], b_yoffb], writes=[b_pY])
                        pump()
                        add("dve", tt_fn(yg[i2], pY[:], sz[:, tb * 512:(tb + 1) * 512], ALU.mult), reads=[b_pY, b_sz[tb]], writes=[b_yg[i2]])
                        add("act", act_fn(sqj, yg[i2], AF.Square, accum_out=ssq[:, tb, g:g + 1]), reads=[b_yg[i2]], writes=[b_sqj, b_ssq[tb]])
                        ph = pT[:, 0:512]

                        def try_(e, i2=i2, ph=ph):
                            ins = None
                            for c in range(4):
                                ins = e.transpose(out=ph[:, c * 128:(c + 1) * 128], in_=yg[i2][:, c * 128:(c + 1) * 128],
                                                  identity=cmb[:, IDENT, :])
                            return ins
                        add("pe", try_, reads=[b_yg[i2], b_cmb], writes=[b_pT[0]])
                        add("dve", tt_fn(ygT[:, g * 4:g * 4 + 4, tsl], ph.rearrange("p (a b) -> p a b", a=4),
                                         ngf[:, g * 4:g * 4 + 4].unsqueeze(2).to_broadcast([128, 4, 128]), ALU.mult),
                            reads=[b_pT[0], b_ng], writes=b_ygT[g * 4:g * 4 + 4])
                    add("dve", tt_fn(xw[i2].rearrange("p (h q) -> p h q", h=8), xdt[i2].rearrange("p (h q) -> p h q", h=8),
                                     dtf[:, tb, WEXP, hs].unsqueeze(2).to_broadcast([128, 8, 64]), ALU.mult),
                        reads=[b_xdt[i2], bd[WEXP]], writes=[b_xw[i2]])
                    add("pe", mm_fn(pO[:], [(Btok[:, tb * 128:(tb + 1) * 128], xw[i2])]), reads=[b_Btok[tb], b_xw[i2]], writes=[b_pO])
                    if pre:
                        pump()
                    sv = S[:, g, :].rearrange("p (h q) -> p h q", h=8)
                    add("dve", tt_fn(sv, sv, dtf[:, tb, EL, hs].unsqueeze(2).to_broadcast([128, 8, 64]), ALU.mult),
                        reads=[b_S[g], bd[EL]], writes=[b_S[g]])
                    add("dve", tt_fn(S[:, g, :], S[:, g, :], pO[:], ALU.add), reads=[b_S[g], b_pO], writes=[b_S[g]])
                    if not pre:
                        copy_on("act", Sbf[:, g & 1, :], S[:, g, :], [b_S[g]], [b_Sbf[g & 1]])
                while pending:
                    pump()
            if pre:
                return
            rst, rbs = [], []
            for tb in range(4):
                sst = small[:, 32 + tb * 2:33 + tb * 2]
                rs = small[:, 33 + tb * 2:34 + tb * 2]
                b_sst, b_rs = b_small[32 + tb * 2], b_small[33 + tb * 2]
                add("dve", lambda e, sst=sst, tb=tb: e.tensor_reduce(out=sst, in_=ssq[:, tb, :], axis=mybir.AxisListType.X, op=ALU.add),
                    reads=[b_ssq[tb]], writes=[b_sst])
                add("act", act_fn(rs, sst, AF.Sqrt, scale=1.0 / DIN, bias=epsb[:, 0:1]), reads=[b_sst, b_eps], writes=[b_rs])
                add("dve", lambda e, rs=rs: e.reciprocal(out=rs, in_=rs), reads=[b_rs], writes=[b_rs])
                rst.append(rs)
                rbs.append(b_rs)
            out_proj(w_out0, rst, rbs)


        xact_t = SB("xact4", [128, 2, 4, 512], BF16)
        xact42 = [[xact_t[:, p_, c, :] for c in range(4)] for p_ in range(2)]
        b_xact42 = [[Buf("xact4_%d_%d" % (p_, c)) for c in range(4)] for p_ in range(2)]

        def layer1(mode, first_tile):
            halo_only = (mode == 'm0')
            norm_to_hnT(1)
            ub = {"i": 0}
            for pg in range(4):
                win = 2 << pg
                for j in range(2):
                    c0 = pg * 1024 + j * 512
                    v0, r0 = load_slab(w_in1[0:1024, c0:c0 + 512], 8, 512)
                    v1, r1 = load_slab(w_in1[1024:2048, c0:c0 + 512], 8, 512)
                    for c in range(4):
                        cblk = j * 4 + c
                        ubi = pg * 8 + cblk
                        i = ub["i"] & 1
                        ub["i"] += 1
                        pp, bpp = acc_banks[ub["i"] & 3]
                        pairs = [((v0 if kc < 8 else v1)[:, kc % 8, c * 128:(c + 1) * 128], hnT[:, kc, :]) for kc in range(16)]
                        add("pe", mm_fn(pp[:], pairs), reads=b_hnT + [r0, r1], writes=[bpp])
                        u = uraw[i]
                        add("dve", lambda e, u=u, ubi=ubi: e.tensor_copy(out=u[:, 0:15], in_=uhalo[:, ubi, :]),
                            reads=[b_uhalo[ubi]], writes=[b_uraw[i]])
                        add("act", act_fn(u[:, 15:527], pp[:], AF.Copy), reads=[bpp], writes=[b_uraw[i]])
                        add("dve", lambda e, u=u, ubi=ubi: e.tensor_copy(out=uhalo[:, ubi, :], in_=u[:, 512:527]),
                            reads=[b_uraw[i]], writes=[b_uhalo[ubi]])
                        if halo_only:
                            continue
                        lo2 = 15 - (win - 2)
                        add("dve", tt_fn(s2[:, lo2:527], u[:, lo2:527], u[:, lo2 - 1:526], ALU.add), reads=[b_uraw[i]], writes=[b_s2])
                        cur, bcur = s2, b_s2
                        if win >= 4:
                            lo4 = 15 - (win - 4)
                            add("dve", tt_fn(s4[:, lo4:527], s2[:, lo4:527], s2[:, lo4 - 2:525], ALU.add), reads=[b_s2], writes=[b_s4])
                            cur, bcur = s4, b_s4
                        if win >= 8:
                            lo8 = 15 - (win - 8)
                            add("dve", tt_fn(s8[:, lo8:527], s4[:, lo8:527], s4[:, lo8 - 4:523], ALU.add), reads=[b_s4], writes=[b_s8])
                            cur, bcur = s8, b_s8
                        if win >= 16:
                            add("dve", tt_fn(s2[:, 15:527], s8[:, 15:527], s8[:, 7:519], ALU.add), reads=[b_s8], writes=[b_s2])
                            cur, bcur = s2, b_s2
                        mo = mixT[:, cblk * 512:(cblk + 1) * 512]
                        add("dve", stt_fn(mo, cur[:, 15:527], 1.0 / win, u[:, 15:527], ALU.mult, ALU.subtract),
                            reads=[bcur, b_uraw[i]], writes=[b_mixT[cblk]])
                        if first_tile:
                            add("dve", tt_fn(utmp, cur[:, 15:31], invc[:, pg, :], ALU.mult), reads=[bcur, b_invc], writes=[b_utmp])
                            add("dve", tt_fn(mo[:, 0:16], utmp, u[:, 15:31], ALU.subtract),
                                reads=[b_utmp, b_uraw[i], b_mixT[cblk]], writes=[b_mixT[cblk]])
                if halo_only:
                    continue
                for j in range(2):
                    c0 = 4096 + pg * 1024 + j * 512
                    v0, r0 = load_slab(w_in1[0:1024, c0:c0 + 512], 8, 512)
                    v1, r1 = load_slab(w_in1[1024:2048, c0:c0 + 512], 8, 512)
                    for c in range(4):
                        cblk = j * 4 + c
                        ub["i"] += 1
                        pp, bpp = acc_banks[ub["i"] & 3]
                        pairs = [((v0 if kc < 8 else v1)[:, kc % 8, c * 128:(c + 1) * 128], hnT[:, kc, :]) for kc in range(16)]
                        add("pe", mm_fn(pp[:], pairs), reads=b_hnT + [r0, r1], writes=[bpp])
                        add("act", act_fn(sgT[:, cblk * 512:(cblk + 1) * 512], pp[:], AF.Silu), reads=[bpp], writes=[b_sgT[cblk]])
                for j in range(2):
                    vg, rg = load_slab(w_grp[pg * 1024:(pg + 1) * 1024, j * 512:(j + 1) * 512], 8, 512)
                    for c in range(4):
                        dblk = j * 4 + c
                        ub["i"] += 1
                        pp, bpp = acc_banks[ub["i"] & 3]
                        pairs = [(vg[:, kc, c * 128:(c + 1) * 128], mixT[:, kc * 512:(kc + 1) * 512]) for kc in range(8)]
                        add("pe", mm_fn(pp[:], pairs), reads=b_mixT + [rg], writes=[bpp])
                        kk = pg * 8 + dblk
                        add("dve", stt_fn(ygT[:, kk, :], pp[:], pscf[:, kk:kk + 1], sgT[:, dblk * 512:(dblk + 1) * 512], ALU.mult, ALU.mult),
                            reads=[bpp, b_psc, b_sgT[dblk]], writes=[b_ygT[kk]])
            if not halo_only:
                out_proj(w_out1, None, None)

        def final_store(t):
            to = t - (NPRE + 1 if NPRE > 0 else 0)
            for tb in range(4):
                ss = small[:, 48 + tb * 2:49 + tb * 2]
                rs = small[:, 49 + tb * 2:50 + tb * 2]
                b_ss, b_rs = b_small[48 + tb * 2], b_small[49 + tb * 2]
                if final:
                    add("dve", stt_fn(hnst[:], xres[:, tb, :], 1.0, xres[:, tb, :], ALU.mult, ALU.mult, accum_out=ss),
                        reads=[b_xres[tb]], writes=[b_hnst, b_ss])
                    add("act", act_fn(rs, ss, AF.Sqrt, scale=1.0 / D, bias=epsb[:, 0:1]), reads=[b_ss, b_eps], writes=[b_rs])
                    add("dve", lambda e, rs=rs: e.reciprocal(out=rs, in_=rs), reads=[b_rs], writes=[b_rs])
                    add("dve", stt_fn(xres[:, tb, :], xres[:, tb, :], rs, fgrow[:], ALU.mult, ALU.mult),
                        reads=[b_xres[tb], b_rs, b_fg], writes=[b_xres[tb]])
                r0 = to * TT + tb * 128
                add("sp", lambda e, tb=tb, r0=r0: e.dma_start(out=out_d[r0:r0 + 128, :], in_=xres[:, tb, :]),
                    reads=[b_xres[tb]], dma=b_xres[tb])

        for t in range(NT):
            for tb in range(4):
                r0 = t * TT + tb * 128
                add("sp", lambda e, tb=tb, r0=r0: e.dma_start(out=xres[:, tb, :], in_=x_d[r0:r0 + 128, :]),
                    writes=[b_xres[tb]], dma=b_xres[tb])
            mode = 'main'
            if NPRE > 0:
                mode = 'pre' if t < NPRE else ('m0' if t == NPRE else 'main')
            first_real = (t == (NPRE + 1 if NPRE > 0 else 0))
            P.handoff(L1_BUFS, L0_BUFS)
            layer0(mode)
            if mode != 'pre':
                P.handoff(L0_BUFS, L1_BUFS)
                layer1(mode, first_real)
            if mode == 'm0':
                fl = flag[:, 0:1]
                for g in range(NG):
                    add("dve", ts_fn(S[:, g, :], S[:, g, :], fl, None, ALU.mult), reads=[b_S[g], b_flag], writes=[b_S[g]])
                add("dve", ts_fn(chalo[:].rearrange("p a b -> p (a b)"), chalo[:].rearrange("p a b -> p (a b)"), fl, None, ALU.mult),
                    reads=b_chalo + [b_flag], writes=b_chalo)
                add("dve", ts_fn(uhalo[:].rearrange("p a b -> p (a b)"), uhalo[:].rearrange("p a b -> p (a b)"), fl, None, ALU.mult),
                    reads=b_uhalo + [b_flag], writes=b_uhalo)
            if mode == 'main':
                final_store(t)
        add("sp", None, writes=b_xres)
        P.emit()
        print("ops:", {e: len(P.ops[e]) for e in ENGS}, "scratch", l0_end, off["v"], flush=True)
    return nc


def _consts(first_half=True):
    t = np.arange(128)
    ident = (t[:, None] == t[None, :]).astype(np.float32)
    ones = np.ones((128, 128), np.float32)
    ntri = -(t[:, None] <= t[None, :]).astype(np.float32)
    trile = (t[:, None] <= t[None, :]).astype(np.float32)
    trigt = (t[:, None] > t[None, :]).astype(np.float32)
    nmask = np.where(t[:, None] > t[None, :], NEG, 0.0).astype(np.float32)
    cmask = np.stack([ident, ones, ntri, trile, trigt, nmask], axis=1)
    invc = np.zeros((1, 4, 16), np.float32)
    for gi, w in enumerate((2, 4, 8, 16)):
        invc[0, gi] = 1.0 / (np.minimum(np.arange(1, 17), w) if first_half else np.full(16, w))
    return np.ascontiguousarray(cmask), invc.reshape(1, 64)


def _fm(v, nblk):
    return np.ascontiguousarray(np.asarray(v, np.float32).reshape(nblk, 128).T)


_NC_CACHE = {}
HALF = SEQ // 2


def core_inputs(x, common, b, h):
    m = dict(common)
    if h == 0:
        m["x"] = np.concatenate([np.zeros((HALF, D), np.float32), x[b, :HALF]], axis=0)
    else:
        m["x"] = np.ascontiguousarray(x[b])
    cm, invc = _consts(first_half=(h == 0))
    m["cmask"] = cm
    m["invc"] = invc
    m["flag"] = np.full((128, 1), float(h), np.float32)
    return m


def common_inputs(ln_g, final_g, ssm_w_in, ssm_conv_w, ssm_conv_b, ssm_dt_bias, ssm_a_log, ssm_d, ssm_norm_g,
                  ssm_w_out, pool_w_in, pool_w_group, pool_scale, pool_w_out):
    f = lambda a: np.ascontiguousarray(np.asarray(a, dtype=np.float32))
    convw = np.ascontiguousarray(f(ssm_conv_w)[0].T.reshape(48, 128, 4).transpose(1, 0, 2))
    convb = _fm(f(ssm_conv_b)[0], 48)
    rows = np.concatenate([f(ssm_dt_bias)[0], f(ssm_a_log)[0], f(ssm_d)[0]])[None, :]
    lng = np.ascontiguousarray(f(ln_g).reshape(2, 16, 128).transpose(2, 0, 1))
    return {
        "ssm_w_in": f(ssm_w_in)[0], "ssm_w_out": f(ssm_w_out)[0], "pool_w_in": f(pool_w_in)[0],
        "pool_w_group": np.ascontiguousarray(f(pool_w_group)[0].reshape(4096, 1024)), "pool_w_out": f(pool_w_out)[0],
        "convw": convw, "convb": convb, "rows": np.ascontiguousarray(rows),
        "lng": lng, "fg": f(final_g)[None, :], "ng": _fm(f(ssm_norm_g)[0], 32), "psc": _fm(f(pool_scale)[0], 32),
    }


def kernel(x, ln_g, final_g, ssm_w_in, ssm_conv_w, ssm_conv_b, ssm_dt_bias, ssm_a_log, ssm_d, ssm_norm_g,
           ssm_w_out, pool_w_in, pool_w_group, pool_scale, pool_w_out):
    x = np.ascontiguousarray(np.asarray(x, dtype=np.float32))
    if "nc" not in _NC_CACHE:
        _NC_CACHE["nc"] = build_program(3, 5)
    nc = _NC_CACHE["nc"]
    common = common_inputs(ln_g, final_g, ssm_w_in, ssm_conv_w, ssm_conv_b, ssm_dt_bias, ssm_a_log, ssm_d, ssm_norm_g,
                           ssm_w_out, pool_w_in, pool_w_group, pool_scale, pool_w_out)
    in_maps = [core_inputs(x, common, c // 2, c % 2) for c in range(8)]
    res = run_bass_kernel_spmd(nc, in_maps, core_ids=list(range(8)))
    out = np.empty((BATCH, SEQ, D), np.float32)
    for c in range(8):
        out[c // 2, (c % 2) * HALF:(c % 2 + 1) * HALF] = np.asarray(res.results[c]["out"], dtype=np.float32)
    return out
```
